# Optimizing a Trainium2 kernel written in Bass

```python
import math
import jax, jax.numpy as jnp
from jax import lax
import numpy as np

D_MODEL = 1024
BATCH = 2
SEQ = 16384
DEPTH = 1

N_META = 16
GRID_W = 64
HEAD_DIM = 128
N_Q_HEADS = 4
N_KV_HEADS = 2
Q_GROUP = N_Q_HEADS // N_KV_HEADS
ATTN_W = N_Q_HEADS * HEAD_DIM
KV_W = N_KV_HEADS * HEAD_DIM
N_FOURIER_GROUPS = 4
FOURIER_GROUP_W = 128
FOURIER_W = N_FOURIER_GROUPS * FOURIER_GROUP_W
MIX_W = ATTN_W + FOURIER_W
IN_PROJ_W = ATTN_W + 2 * KV_W + FOURIER_W
D_FF = 4 * D_MODEL
Q_BLOCK = 128
ROPE_THETA = 10000.0
ROPE_AXIS_DIM = HEAD_DIM // 2
RMS_EPS = 1e-6

kernel_name = 'hymba_axial_gqa_fnet_encoder_block'


def rms_norm(x, g):
    xf = x.astype(jnp.float32)
    y = xf * lax.rsqrt(jnp.mean(xf * xf, axis=-1, keepdims=True) + RMS_EPS)
    return (y * g.astype(jnp.float32)).astype(x.dtype)


def grid_positions(n_tok):
    rows_count = n_tok // GRID_W
    real_row = jnp.repeat(jnp.arange(rows_count, dtype=jnp.float32), GRID_W)
    real_col = jnp.tile(jnp.arange(GRID_W, dtype=jnp.float32), rows_count)
    meta_row = jnp.full((N_META,), -1.0, dtype=jnp.float32)
    meta_col = jnp.arange(N_META, dtype=jnp.float32)
    return jnp.concatenate([meta_row, real_row]), jnp.concatenate([meta_col, real_col])


def rope_angles(pos):
    inv_freq = ROPE_THETA ** (-jnp.arange(0, ROPE_AXIS_DIM, 2, dtype=jnp.float32) / ROPE_AXIS_DIM)
    ang = pos[:, None] * inv_freq[None, :]
    return jnp.cos(ang), jnp.sin(ang)


def _rotate(x, cos, sin):
    c = cos[None, :, None, :]
    s = sin[None, :, None, :]
    x1, x2 = jnp.split(x, 2, axis=-1)
    return jnp.concatenate([x1 * c - x2 * s, x2 * c + x1 * s], axis=-1)


def axial_rope(x, cos_r, sin_r, cos_c, sin_c):
    xf = x.astype(jnp.float32)
    xr = _rotate(xf[..., :ROPE_AXIS_DIM], cos_r, sin_r)
    xc = _rotate(xf[..., ROPE_AXIS_DIM:], cos_c, sin_c)
    return jnp.concatenate([xr, xc], axis=-1).astype(x.dtype)


def block_attention(q, k, v):
    b, l, _, d = q.shape
    scale = 1.0 / math.sqrt(d)
    qt = q.reshape(b, l, N_KV_HEADS, Q_GROUP, d).transpose(0, 2, 3, 1, 4)
    kt = k.transpose(0, 2, 1, 3)
    vt = v.transpose(0, 2, 1, 3)

    def attend(qb):
        s = jnp.einsum('bkgqd,bksd->bkgqs', qb, kt).astype(jnp.float32) * scale
        p = jax.nn.softmax(s, axis=-1).astype(vt.dtype)
        return jnp.einsum('bkgqs,bksd->bkgqd', p, vt)

    out_meta = attend(qt[:, :, :, :N_META])
    n_real = l - N_META
    nb = n_real // Q_BLOCK
    qr = qt[:, :, :, N_META:].reshape(b, N_KV_HEADS, Q_GROUP, nb, Q_BLOCK, d)
    qr = jnp.moveaxis(qr, 3, 0)
    out_real = lax.map(attend, qr)
    out_real = jnp.moveaxis(out_real, 0, 3).reshape(b, N_KV_HEADS, Q_GROUP, n_real, d)
    out = jnp.concatenate([out_meta, out_real], axis=3)
    return out.transpose(0, 3, 1, 2, 4).reshape(b, l, ATTN_W)


def fourier_mix(u, w_f):
    b, l, _ = u.shape
    ug = u.reshape(b, l, N_FOURIER_GROUPS, FOURIER_GROUP_W).astype(jnp.float32)
    f = jnp.fft.fft2(ug, axes=(1, 3), norm='ortho').real.astype(u.dtype)
    y = jnp.einsum('blgc,gcd->blgd', f, w_f)
    return y.reshape(b, l, FOURIER_W)


def setup_inputs(seed: int = 0) -> dict:
    key = jax.random.key(seed)
    ks = jax.random.split(key, 16)
    f32 = jnp.float32

    def gain(k, shape):
        return 1.0 + 0.02 * jax.random.normal(k, shape, f32)

    x = jax.random.normal(ks[0], (BATCH, SEQ, D_MODEL), f32)
    meta_tokens = jax.random.normal(ks[1], (N_META, D_MODEL), f32)
    g_mix = gain(ks[2], (DEPTH, D_MODEL))
    w_in = jax.random.normal(ks[3], (DEPTH, D_MODEL, IN_PROJ_W), f32) * D_MODEL ** -0.5
    g_q = gain(ks[4], (DEPTH, HEAD_DIM))
    g_k = gain(ks[5], (DEPTH, HEAD_DIM))
    w_fourier = jax.random.normal(ks[6], (DEPTH, N_FOURIER_GROUPS, FOURIER_GROUP_W, FOURIER_GROUP_W), f32) * FOURIER_GROUP_W ** -0.5
    g_attn_out = gain(ks[7], (DEPTH, ATTN_W))
    g_fourier_out = gain(ks[8], (DEPTH, FOURIER_W))
    w_out = jax.random.normal(ks[9], (DEPTH, MIX_W, D_MODEL), f32) * MIX_W ** -0.5
    g_mlp = gain(ks[10], (DEPTH, D_MODEL))
    w_up = jax.random.normal(ks[11], (DEPTH, D_MODEL, D_FF), f32) * D_MODEL ** -0.5
    w_down = jax.random.normal(ks[12], (DEPTH, D_FF, D_MODEL), f32) * D_FF ** -0.5
    g_final = gain(ks[13], (D_MODEL,))
    return {'x': x, 'meta_tokens': meta_tokens, 'g_mix': g_mix, 'w_in': w_in, 'g_q': g_q, 'g_k': g_k,
            'w_fourier': w_fourier, 'g_attn_out': g_attn_out, 'g_fourier_out': g_fourier_out, 'w_out': w_out,
            'g_mlp': g_mlp, 'w_up': w_up, 'w_down': w_down, 'g_final': g_final}


def reference(x, meta_tokens, g_mix, w_in, g_q, g_k, w_fourier, g_attn_out, g_fourier_out, w_out,
              g_mlp, w_up, w_down, g_final):
    b, n_tok, d = x.shape
    meta = jnp.broadcast_to(meta_tokens[None].astype(x.dtype), (b, N_META, d))
    h = jnp.concatenate([meta, x], axis=1)
    l = h.shape[1]

    row, col = grid_positions(n_tok)
    cos_r, sin_r = rope_angles(row)
    cos_c, sin_c = rope_angles(col)

    for i in range(DEPTH):
        hn = rms_norm(h, g_mix[i])
        proj = hn @ w_in[i]
        q = proj[..., :ATTN_W].reshape(b, l, N_Q_HEADS, HEAD_DIM)
        k = proj[..., ATTN_W:ATTN_W + KV_W].reshape(b, l, N_KV_HEADS, HEAD_DIM)
        v = proj[..., ATTN_W + KV_W:ATTN_W + 2 * KV_W].reshape(b, l, N_KV_HEADS, HEAD_DIM)
        u = proj[..., ATTN_W + 2 * KV_W:]
        q = axial_rope(rms_norm(q, g_q[i]), cos_r, sin_r, cos_c, sin_c)
        k = axial_rope(rms_norm(k, g_k[i]), cos_r, sin_r, cos_c, sin_c)
        attn = block_attention(q, k, v)
        four = fourier_mix(u, w_fourier[i])
        mixed = jnp.concatenate([rms_norm(attn, g_attn_out[i]), rms_norm(four, g_fourier_out[i])], axis=-1)
        h = h + mixed @ w_out[i]
        m = rms_norm(h, g_mlp[i])
        h = h + jnp.square(jax.nn.relu(m @ w_up[i])) @ w_down[i]

    return rms_norm(h, g_final)[:, N_META:]
```

```python
import math
import os
from contextlib import ExitStack

import numpy as np
import ml_dtypes

import concourse.bass as bass
import concourse.mybir as mybir
from concourse.bass_utils import run_bass_kernel_spmd

F32 = mybir.dt.float32
BF16 = mybir.dt.bfloat16
ALU = mybir.AluOpType
AF = mybir.ActivationFunctionType

D_MODEL = 1024
SEQ = 16384
N_META = 16
LTOT = SEQ + N_META
NKT = 129
OWN = 4096
EPS = 1e-6
N1 = 100
N2 = 164
T2 = 41
MW = 3 * T2
PTW = 4200


class _Op:
    __slots__ = ("eng", "fn", "deps", "is_dma", "semkey", "val", "signal")


class Sched:
    ENGS = ("pe", "act", "dve", "pool", "sp")
    BLK = {"pe": "tensor", "act": "scalar", "dve": "vector", "pool": "gpsimd", "sp": "sync"}

    def __init__(self, nc, stack):
        self.nc = nc
        self.stack = stack
        self.sems = {}
        self.count = {}
        self.waited = {e: {} for e in self.ENGS}
        self.prev = {}
        self._reset()

    def _reset(self):
        self.ops = {e: [] for e in self.ENGS}
        self.bufs = {}

    def _sem(self, key):
        if key not in self.sems:
            self.sems[key] = self.stack.enter_context(self.nc.semaphore("s%d" % len(self.sems)))
            self.count[key] = 0
        return self.sems[key]

    def add(self, eng, fn, reads=(), writes=(), dma_key=None):
        op = _Op()
        op.eng = eng
        op.fn = fn
        op.deps = []
        op.is_dma = dma_key is not None
        op.signal = op.is_dma
        op.val = None
        op.semkey = dma_key if op.is_dma else eng
        self._sem(op.semkey)
        if op.is_dma:
            self.count[dma_key] += 16
            op.val = self.count[dma_key]
        for b in reads:
            st = self.bufs.setdefault(b, [None, []])
            if st[0] is not None:
                op.deps.append((st[0], "raw"))
            name = b if isinstance(b, str) else b[0]
            if name.startswith("ps"):
                for r in st[1]:
                    if r.eng != eng:
                        op.deps.append((r, "xrr"))
            st[1].append(op)
        for b in writes:
            st = self.bufs.setdefault(b, [None, []])
            if st[0] is not None:
                op.deps.append((st[0], "waw"))
            for r in st[1]:
                if r is not op:
                    op.deps.append((r, "war"))
            st[0] = op
            st[1] = []
        self.ops[eng].append(op)
        return op

    @staticmethod
    def _needs_wait(op, d, kind):
        if d.is_dma:
            return True
        if d.eng == op.eng:
            return op.eng != "pe"
        return True

    def emit(self):
        nc = self.nc
        for e in self.ENGS:
            for op in self.ops[e]:
                for d, kind in op.deps:
                    if self._needs_wait(op, d, kind):
                        d.signal = True
            if e != "sp" and self.ops[e]:
                self.ops[e][-1].signal = True
        barrier = self.prev
        for e in self.ENGS:
            if e == "sp":
                continue
            self._sem(e)
            c = self.count[e]
            for op in self.ops[e]:
                if not op.is_dma and op.signal:
                    c += 1
                    op.val = c
            self.count[e] = c
        self.prev = dict(self.count)
        with nc.Block() as block:
            for e in self.ENGS:
                def body(eng, e=e):
                    waited = self.waited[e]
                    for key, v in barrier.items():
                        if key == e or v == 0:
                            continue
                        if waited.get(key, 0) < v:
                            eng.wait_ge(self.sems[key], v)
                            waited[key] = v
                    for op in self.ops[e]:
                        for d, kind in op.deps:
                            if not self._needs_wait(op, d, kind):
                                continue
                            if waited.get(d.semkey, 0) < d.val:
                                eng.wait_ge(self.sems[d.semkey], d.val)
                                waited[d.semkey] = d.val
                        ins = op.fn(eng)
                        if op.is_dma:
                            ins.then_inc(self.sems[op.semkey], 16)
                        elif op.signal:
                            ins.then_inc(self.sems[e], 1)
                getattr(block, self.BLK[e])(body)
        self._reset()

    def final_fence(self):
        nc = self.nc
        barrier = dict(self.count)
        with nc.Block() as block:
            for e in self.ENGS:
                def body(eng, e=e):
                    for key, v in barrier.items():
                        if key == e or v == 0:
                            continue
                        if self.waited[e].get(key, 0) < v:
                            eng.wait_ge(self.sems[key], v)
                            self.waited[e][key] = v
                getattr(block, self.BLK[e])(body)


def _ap(base, dims, extra_off=0):
    return bass.AP(base.tensor, base.offset + extra_off, [list(base.ap[0])] + [list(d) for d in dims])


def build_program(debug=None):
    nc = bass.Bass("TRN2", target_bir_lowering=False)
    I = {}

    def din(name, shape, dt):
        I[name] = nc.dram_tensor(name, list(shape), dt, kind="ExternalInput").ap()

    din("xb", [SEQ, 1024], F32)
    din("meta", [16, 1024], F32)
    din("xown", [OWN, 1024], F32)
    din("w_in", [1024, 1536], F32)
    din("w_out", [1024, 1024], F32)
    din("w_up", [1024, 4096], F32)
    din("w_down", [4096, 1024], F32)
    din("g_mix", [1024], F32)
    din("g_q", [128], F32)
    din("g_k", [128], F32)
    din("g_ao", [512], F32)
    din("g_fo", [512], F32)
    din("g_mlp", [1024], F32)
    din("g_final", [1024], F32)
    din("w_four", [4, 128, 128], F32)
    din("ropek", [NKT * 128, 256], F32)
    din("ropeq", [OWN, 256], F32)
    din("ident", [128, 128], BF16)
    din("dftw1", [N1, 2 * N1], BF16)
    din("dftm", [2, 82, N1 * MW], BF16)
    din("dftc", [128, 256], BF16)
    out = nc.dram_tensor("out", [OWN, 1024], F32, kind="ExternalOutput").ap()
    skind = "ExternalOutput" if debug else "Internal"
    u_dram = nc.dram_tensor("u_dram", [512, LTOT], BF16, kind=skind).ap()
    q_dram = nc.dram_tensor("q_dram", [128, 4, OWN], BF16, kind=skind).ap()
    ma_dram = nc.dram_tensor("ma_dram", [128, 4, OWN], BF16, kind=skind).ap()
    mf_dram = nc.dram_tensor("mf_dram", [128, 4, OWN], BF16, kind=skind).ap()
    dbg = {}
    if debug:
        dbg["kt"] = nc.dram_tensor("dbg_kt", [128, 2 * LTOT], BF16, kind="ExternalOutput").ap()
        dbg["v"] = nc.dram_tensor("dbg_v", [128, NKT * 2 * 129], BF16, kind="ExternalOutput").ap()
        dbg["st"] = nc.dram_tensor("dbg_st", [128, 128], F32, kind="ExternalOutput").ap()

    with ExitStack() as top:
        S = Sched(nc, top)

        uniq = [0]

        def sb(stack, name, shape, dt):
            uniq[0] += 1
            return stack.enter_context(nc.sbuf_tensor("%s_%d" % (name, uniq[0]), list(shape), dt))

        def ps(stack, name, shape, dt):
            uniq[0] += 1
            return stack.enter_context(nc.psum_tensor("%s_%d" % (name, uniq[0]), list(shape), dt))

        def DMA(o, i, reads, writes, key, **kw):
            S.add("sp", lambda e: e.dma_start(out=o, in_=i, **kw), reads, writes, dma_key=key)

        identb = sb(top, "identb", [128, 128], BF16)
        epsc = sb(top, "epsc", [128, 1], F32)
        junk_t = sb(top, "junk", [128, 4, 1024], BF16)
        jstate = [0]

        def jnk(n=1024):
            k = jstate[0] % 4
            jstate[0] += 1
            return junk_t[:, k, 0:n], ("junk", k)
        ssa = sb(top, "ssa", [128, 32], F32)
        ssfp = sb(top, "ssfp", [128, 4, 32], F32)
        DMA(identb[:], I["ident"], [], ["identb"], "identb")
        S.add("pool", lambda e: e.memset(epsc[:], EPS), [], ["epsc"])
        if debug:
            S.add("pool", lambda e: e.memset(ssa[:], 1.0), [], ["ssa_init"])

        def rms_rstd(ss_ap, sd_ap, rs_ap, n, kss, ksd, krs):
            S.add("act", lambda e: e.activation(out=sd_ap, in_=ss_ap, func=AF.Sqrt, bias=epsc[:, 0:1],
                                                scale=1.0 / n), (list(kss) if isinstance(kss, list) else [kss]) + ["epsc"], [ksd])
            S.add("dve", lambda e: e.reciprocal(out=rs_ap, in_=sd_ap), [ksd], [krs])

        st_ab = top.enter_context(ExitStack())
        KT = sb(st_ab, "KT", [128, 2, LTOT], BF16)
        VE = sb(st_ab, "VE", [128, NKT, 2, 129], BF16)
        st_a = top.enter_context(ExitStack())
        w_in_sb = sb(st_a, "w_in_sb", [128, 8, 1536], BF16)
        gq_b = sb(st_a, "gq_b", [128, 128], F32)
        gk_b = sb(st_a, "gk_b", [128, 128], F32)
        gmT = sb(st_a, "gmT", [128, 8], F32)

        with ExitStack() as st:
            wst = [sb(st, "wst%d" % i, [128, 768], F32) for i in range(2)]
            DMA(gmT[:], I["g_mix"].rearrange("(c p) -> p c", p=128), [], ["gmT"], "gmT",
                allow_slow_non_contiguous=True)
            DMA(gq_b[:], I["g_q"].partition_broadcast(128), [], ["gq_b"], "gq_b")
            DMA(gk_b[:], I["g_k"].partition_broadcast(128), [], ["gk_b"], "gk_b")
            S.add("pool", lambda e: e.memset(VE[:, :, :, 128:129], 1.0), [], ["VE1"])
            k = 0
            for c in range(8):
                for hf in range(2):
                    s = k % 2
                    DMA(wst[s][:], I["w_in"][c * 128:(c + 1) * 128, hf * 768:(hf + 1) * 768],
                        [], [("wst", s)], ("wst", s))
                    eng = "dve" if k % 2 == 0 else "pool"
                    S.add(eng, lambda e, s=s, c=c, hf=hf: e.tensor_scalar(
                        out=w_in_sb[:, c, hf * 768:(hf + 1) * 768], in0=wst[s][:],
                        scalar1=gmT[:, c:c + 1], scalar2=None, op0=ALU.mult),
                        [("wst", s), "gmT"], [("w_in_sb", c, hf)])
                    k += 1
            S.emit()

        if debug == "P0":
            DMA(dbg["kt"][:, 0:12288], w_in_sb[:].rearrange("p a b -> p (a b)"), [], ["dbgkt"], "dbgkt")
            DMA(dbg["st"], gq_b[:], [], ["dbgst"], "dbgst")
            S.emit()
            S.final_fence()
            return nc

        def alloc_front(st):
            fb = {}
            fb["xt"] = [sb(st, "xt%d" % i, [128, 1024], F32) for i in range(3)]
            fb["xn"] = [sb(st, "xn%d" % i, [128, 1024], BF16) for i in range(3)]
            fb["xnT"] = [sb(st, "xnT%d" % i, [128, 8, 128], BF16) for i in range(3)]
            fb["ss"] = [sb(st, "ss%d" % i, [128, 1], F32) for i in range(2)]
            fb["sd"] = [sb(st, "sd%d" % i, [128, 1], F32) for i in range(2)]
            fb["rs"] = [sb(st, "rs%d" % i, [128, 1], F32) for i in range(2)]
            fb["psT"] = [ps(st, "psT%d" % i, [128, 1024], BF16) for i in range(2)]
            return fb

        def front0(fb, i, src_rows=None, meta=False):
            x3 = i % 3
            xt = fb["xt"][x3]
            if meta:
                S.add("pool", lambda e: e.memset(xt[:], 0.0), [], [("xt", x3)])
                DMA(xt[0:16, :], I["meta"], [], [("xt", x3)], ("xt", x3))
            else:
                DMA(xt[:], src_rows, [], [("xt", x3)], ("xt", x3))

        def front1(fb, i):
            s, s3 = i % 2, i % 3
            ss, sd, rs = (fb[k][s] for k in ("ss", "sd", "rs"))
            xt = fb["xt"][s3]
            xn = fb["xn"][s3]
            ja, jk = jnk()
            S.add("act", lambda e: e.activation(out=ja, in_=xt[:], func=AF.Square, accum_out=ss[:, 0:1]),
                  [("xt", s3)], [("ss", s), jk])
            rms_rstd(ss[:, 0:1], sd[:, 0:1], rs[:, 0:1], 1024.0, ("ss", s), ("sd", s), ("rs", s))
            S.add("dve", lambda e: e.tensor_scalar(out=xn[:], in0=xt[:], scalar1=rs[:, 0:1], scalar2=None,
                                                   op0=ALU.mult), [("xt", s3), ("rs", s)], [("xn", s3)])

        def front2(fb, i):
            s, s3 = i % 2, i % 3
            xn, xnT, psT = fb["xn"][s3], fb["xnT"][s3], fb["psT"][s]

            def tr(e):
                ins = None
                for c in range(8):
                    ins = e.transpose(out=psT[:, c * 128:(c + 1) * 128], in_=xn[:, c * 128:(c + 1) * 128],
                                      identity=identb[:])
                return ins
            S.add("pe", tr, [("xn", s3), "identb"], [("psT", s)])
            S.add("act", lambda e: e.activation(out=xnT[:].rearrange("p a b -> p (a b)"), in_=psT[:], func=AF.Copy),
                  [("psT", s)], [("xnT", s3)])

        def head_norm_rope(nh, s, psrc, pkey, hb, gb, gkey, rp, tag, rps=0):
            hss, hsd, hrs, hn, t1, t2, kr = (hb[k][s] for k in ("hss", "hsd", "hrs", "hn", "t1", "t2", "kr"))
            W = nh * 128

            HV = os.environ.get("KHV", "psum")
            if HV == "copy":
                S.add("act", lambda e: e.activation(out=hn[:, 0:W], in_=psrc[:, 0:W], func=AF.Copy), [pkey], [(tag, "hraw", s)])
                sq_src, sq_key = hn, (tag, "hraw", s)
            else:
                sq_src, sq_key = psrc, pkey
            for h in range(nh):
                ja, jk = jnk(128)
                acc = hss[:, h:h + 1] if HV != "sep" else hb["hss1"][s][h][:, 0:1]
                S.add("act", lambda e, h=h, ja=ja, acc=acc: e.activation(out=ja, in_=sq_src[:, h * 128:(h + 1) * 128],
                                                                         func=AF.Square, accum_out=acc),
                      [sq_key], [(tag, "hss", s, h), jk])
            HS = os.environ.get("KSKIP", "").split(",")
            if "hsq" in HS:
                return
            rms_rstd(hss[:, 0:nh], hsd[:, 0:nh], hrs[:, 0:nh], 128.0, [(tag, "hss", s, h) for h in range(nh)],
                     (tag, "hsd", s), (tag, "hrs", s))
            if "hrms" in HS:
                return

            def nrm(e):
                ins = None
                for h in range(nh):
                    ins = e.scalar_tensor_tensor(out=hn[:, h * 128:(h + 1) * 128], in0=psrc[:, h * 128:(h + 1) * 128],
                                                 scalar=hrs[:, h:h + 1], in1=gb[:], op0=ALU.mult, op1=ALU.mult)
                return ins
            S.add("dve", nrm, [pkey, (tag, "hrs", s), gkey], [(tag, "hn", s)])
            if "hnrm" in HS:
                return
            RE = os.environ.get("KROPE", "pool")
            cosb = _ap(rp[:, 0:128], [[0, nh], [1, 128]])
            hn3 = _ap(hn[:, 0:W], [[128, nh], [1, 128]])
            t13 = _ap(t1[:, 0:W], [[128, nh], [1, 128]])
            S.add(RE, lambda e: e.tensor_tensor(out=t13, in0=hn3, in1=cosb, op=ALU.mult),
                  [(tag, "hn", s), (tag, "rp", rps)], [(tag, "t1", s)])

            def swp(e):
                ins = None
                for hf in range(2):
                    o = _ap(t2[:, 0:W], [[128, nh], [64, 2], [1, 32]], hf * 32)
                    a = _ap(hn[:, 0:W], [[128, nh], [64, 2], [1, 32]], (1 - hf) * 32)
                    b = _ap(rp[:, 128:256], [[0, nh], [64, 2], [1, 32]], hf * 32)
                    ins = e.tensor_tensor(out=o, in0=a, in1=b, op=ALU.mult)
                return ins
            S.add(RE, swp, [(tag, "hn", s), (tag, "rp", rps)], [(tag, "t2", s)])
            S.add(RE, lambda e: e.tensor_tensor(out=kr[:, 0:W], in0=t1[:, 0:W], in1=t2[:, 0:W], op=ALU.add),
                  [(tag, "t1", s), (tag, "t2", s)], [(tag, "kr", s)])

        def alloc_head(st, nh, tag):
            hb = {}
            W = nh * 128
            for nm, shp, dt in (("hss", [128, 4], F32), ("hsd", [128, 4], F32), ("hrs", [128, 4], F32),
                                ("hn", [128, W], F32), ("t1", [128, W], F32), ("t2", [128, W], F32),
                                ("kr", [128, W], BF16), ("rp", [128, 256], F32)):
                hb[nm] = [sb(st, "%s_%s%d" % (tag, nm, i), shp, dt) for i in range(4 if nm == "rp" else 2)]
            hb["hss1"] = [[sb(st, "%s_hss1_%d_%d" % (tag, i, h), [128, 1], F32) for h in range(nh)] for i in range(2)]
            return hb

        with ExitStack() as st:
            fb = alloc_front(st)
            hb = alloc_head(st, 2, "k")
            ust = [sb(st, "ust%d" % i, [128, 4, 128], BF16) for i in range(2)]
            psKV = [ps(st, "psKV%d" % i, [128, 512], F32) for i in range(2)]
            psU = [ps(st, "psU%d" % i, [128, 512], F32) for i in range(2)]
            psKT = [ps(st, "psKT%d" % i, [128, 1024], BF16) for i in range(2)]
            u_v = u_dram.rearrange("(j p) n -> p j n", p=128)
            ntile = NKT if debug != "A_small" else int(os.environ.get("KNT", "3"))
            SKIP = os.environ.get("KSKIP", "").split(",") if debug else []
            def tile_cols(i):
                return (16, 0) if i == 0 else (128, 16 + (i - 1) * 128)

            def A_F0(i):
                if i == 0:
                    front0(fb, i, meta=True)
                else:
                    front0(fb, i, src_rows=I["xb"][(i - 1) * 128:i * 128, :])
                r4 = i % 4
                DMA(hb["rp"][r4][:], I["ropek"][i * 128:(i + 1) * 128, :], [], [("k", "rp", r4)], ("k", "rp", r4))

            def A_B1(i):
                s, s3 = i % 2, i % 3
                n, c0 = tile_cols(i)
                xnT = fb["xnT"][s3]
                rp = hb["rp"][i % 4]

                def mmkv(e):
                    ins = None
                    for c in range(8):
                        ins = e.matmul(psKV[s][:], lhsT=xnT[:, c, :], rhs=w_in_sb[:, c, 512:1024],
                                       start=(c == 0), stop=(c == 7))
                    return ins
                S.add("pe", mmkv, [("xnT", s3)], [("psKV", s)])

                def mmu(e):
                    ins = None
                    for j in range(4):
                        for c in range(8):
                            ins = e.matmul(psU[s][:, j * 128:(j + 1) * 128],
                                           lhsT=w_in_sb[:, c, 1024 + j * 128:1024 + (j + 1) * 128],
                                           rhs=xnT[:, c, :], start=(c == 0), stop=(c == 7))
                    return ins
                S.add("pe", mmu, [("xnT", s3)], [("psU", s)])
                head_norm_rope(2, s, psKV[s], ("psKV", s), hb, gk_b, "gk_b", rp, "k", rps=i % 4)
                S.add("dve", lambda e: e.tensor_copy(
                    out=VE[:, i, :, 0:128], in_=psKV[s][:, 256:512].rearrange("p (a b) -> p a b", a=2)),
                    [("psKV", s)], [("VE", i)])
                S.add("dve", lambda e: e.tensor_copy(out=ust[s][:].rearrange("p a b -> p (a b)"), in_=psU[s][:]),
                      [("psU", s)], [("ust", s)])
                DMA(u_v[:, :, c0:c0 + n], ust[s][:, :, 0:n], [("ust", s)], [("udram", i)], ("ust", s))

            def A_B2(i):
                s = i % 2
                n, c0 = tile_cols(i)
                kr = hb["kr"][s]

                def trk(e):
                    ins = None
                    for h in range(2):
                        ins = e.transpose(out=psKT[s][:, h * 128:(h + 1) * 128], in_=kr[:, h * 128:(h + 1) * 128],
                                          identity=identb[:])
                    return ins
                S.add("pe", trk, [("k", "kr", s)], [("psKT", s)])
                S.add("dve", lambda e: e.tensor_copy(
                    out=KT[:, :, c0:c0 + n], in_=psKT[s][:, 0:256].rearrange("p (a b) -> p a b", a=2)[:, :, 0:n]),
                    [("psKT", s)], [("KT", i)])

            A_F0(0)
            for t in range(ntile + 3):
                if t + 1 < ntile:
                    A_F0(t + 1)
                if t < ntile:
                    front1(fb, t)
                if 0 <= t - 1 < ntile:
                    front2(fb, t - 1)
                if 0 <= t - 2 < ntile:
                    A_B1(t - 2)
                if 0 <= t - 3 < ntile:
                    A_B2(t - 3)
            if debug in ("A", "A_small"):
                ncol = 16 + (ntile - 1) * 128
                DMA(dbg["kt"].rearrange("p (a b) -> p a b", a=2)[:, :, 0:ncol], KT[:, :, 0:ncol],
                    [("KT", i) for i in range(ntile)], ["dbgkt"], "dbgkt")
                DMA(dbg["v"].rearrange("p (a b) -> p a b", a=NKT)[:, 0:ntile, :],
                    VE[:, 0:ntile, :, :].rearrange("p a b c -> p a (b c)"),
                    [("VE", i) for i in range(ntile)] + ["VE1"], ["dbgv"], "dbgv")
            S.emit()
        if debug in ("A", "A_small"):
            S.final_fence()
            return nc

        with ExitStack() as st:
            fb = alloc_front(st)
            hb = alloc_head(st, 4, "q")
            qst = [sb(st, "qst%d" % i, [128, 4, 128], BF16) for i in range(2)]
            psQ = [ps(st, "psQ%d" % i, [128, 512], F32) for i in range(2)]
            psQT = [ps(st, "psQT%d" % i, [128, 1024], BF16) for i in range(2)]
            def Q_F0(t):
                front0(fb, t, src_rows=I["xown"][t * 128:(t + 1) * 128, :])
                r4 = t % 4
                DMA(hb["rp"][r4][:], I["ropeq"][t * 128:(t + 1) * 128, :], [], [("q", "rp", r4)], ("q", "rp", r4))

            def Q_B1(t):
                s, s3 = t % 2, t % 3
                xnT = fb["xnT"][s3]

                def mmq(e):
                    ins = None
                    for c in range(8):
                        ins = e.matmul(psQ[s][:], lhsT=xnT[:, c, :], rhs=w_in_sb[:, c, 0:512],
                                       start=(c == 0), stop=(c == 7))
                    return ins
                S.add("pe", mmq, [("xnT", s3)], [("psQ", s)])
                head_norm_rope(4, s, psQ[s], ("psQ", s), hb, gq_b, "gq_b", hb["rp"][t % 4], "q", rps=t % 4)

            def Q_B2(t):
                s = t % 2
                kr = hb["kr"][s]

                def trq(e):
                    ins = None
                    for h in range(4):
                        ins = e.transpose(out=psQT[s][:, h * 128:(h + 1) * 128], in_=kr[:, h * 128:(h + 1) * 128],
                                          identity=identb[:])
                    return ins
                S.add("pe", trq, [("q", "kr", s)], [("psQT", s)])
                S.add("dve", lambda e: e.tensor_copy(out=qst[s][:].rearrange("p a b -> p (a b)"), in_=psQT[s][:, 0:512]),
                      [("psQT", s)], [("qst", s)])
                DMA(q_dram[:, :, t * 128:(t + 1) * 128], qst[s][:], [("qst", s)], [("qdram", t // 4)], ("qst", s))

            nq_t = 32 if debug != "C" else 0
            if nq_t:
                Q_F0(0)
            for t in range(nq_t + 3 if nq_t else 0):
                if t + 1 < nq_t:
                    Q_F0(t + 1)
                if t < nq_t:
                    front1(fb, t)
                if 0 <= t - 1 < nq_t:
                    front2(fb, t - 1)
                if 0 <= t - 2 < nq_t:
                    Q_B1(t - 2)
                if 0 <= t - 3 < nq_t:
                    Q_B2(t - 3)
            S.emit()
        st_a.close()
        if debug == "A2":
            S.final_fence()
            return nc

        with ExitStack() as st:
            QT = [sb(st, "QT%d" % i, [128, 4, 512], BF16) for i in range(2)]
            pT = [sb(st, "pT%d" % i, [128, 1024], BF16) for i in range(3)]
            psum_t = [sb(st, "psumt%d" % i, [128, 512], BF16) for i in range(4)]
            acc = [sb(st, "acc%d" % i, [128, 512], F32) for i in range(2)]
            recd = sb(st, "recd", [128, 512], F32)
            asq = [sb(st, "asq%d" % i, [128, 512], BF16) for i in range(2)]
            mast = [sb(st, "mast%d" % i, [128, 4, 512], BF16) for i in range(2)]
            onesf = sb(st, "onesf", [128, 128], F32)
            onesb2 = sb(st, "onesb2", [128, 1], BF16)
            sstmp = sb(st, "sstmp", [128, 16], F32)
            psS = [ps(st, "psS%d" % i, [128, 1024], F32) for i in range(2)]
            psOT = [ps(st, "psOT%d" % i, [128, 512], F32) for i in range(2)]
            psDen = ps(st, "psDen", [128, 512], F32)
            psSSa = ps(st, "psSSa", [128, 512], F32)
            scale = 1.0 / math.sqrt(128.0)
            nqg = 8 if debug not in ("B_small", "D_small") else 1
            if debug == "C":
                nqg = 0
            S.add("pool", lambda e: e.memset(onesf[:], 1.0), [], ["onesf"])
            S.add("pool", lambda e: e.memset(onesb2[:], 1.0), [], ["onesb2"])

            def kcols(kt):
                return (16, 0) if kt == 0 else (128, 16 + (kt - 1) * 128)

            hcnt = [0]
            gstep = [0]
            tslot = [0]

            def do_qg(qg):
                qs = qg % 2
                ms = qg % 2
                DMA(QT[qs][:], q_dram[:, :, qg * 512:(qg + 1) * 512], [("qdram", qg)], [("QT", qs)], ("QT", qs))
                steps = [(h, tiles) for h in range(4) for tiles in ([[0]] + [[a, a + 1] for a in range(1, NKT, 2)])]
                cnt = [0]
                pend = [None]

                def emit_pv(h, tiles, pl, hs):
                    kv = h // 2

                    def pv(e):
                        ins = None
                        for idx, kt in enumerate(tiles):
                            rows, _ = kcols(kt)
                            ins = e.matmul(psOT[hs][:, 0:512], lhsT=VE[0:rows, kt, kv, 0:128],
                                           rhs=pT[pl][0:rows, idx * 512:(idx + 1) * 512],
                                           start=(kt == 0), stop=(kt == NKT - 1))
                        return ins
                    S.add("pe", pv, [("pT", pl)], [("psOT", hs)])
                    if len(tiles) == 1:
                        S.add("dve", lambda e: e.tensor_copy(out=acc[hs][0:16, :], in_=pT[pl][0:16, 0:512]),
                              [("pT", pl), ("acc0", hs)], [("acc", hs)])
                    else:
                        pi = cnt[0] % 2
                        cnt[0] += 1
                        tb = tslot[0]
                        S.add("dve", lambda e: e.tensor_tensor(out=psum_t[tb + pi][:], in0=pT[pl][:, 0:512], in1=pT[pl][:, 512:1024],
                                                               op=ALU.add), [("pT", pl)], [("psumt", tb + pi)])
                        if pi == 1:
                            S.add("dve", lambda e: e.tensor_tensor(out=psum_t[tb][:], in0=psum_t[tb][:], in1=psum_t[tb + 1][:],
                                                                   op=ALU.add), [("psumt", tb), ("psumt", tb + 1)], [("psumt", tb)])
                            S.add("dve", lambda e: e.tensor_tensor(out=acc[hs][:], in0=acc[hs][:], in1=psum_t[tb][:], op=ALU.add),
                                  [("psumt", tb), ("acc", hs)], [("acc", hs)])
                            tslot[0] = 2 - tb
                    if tiles[-1] == NKT - 1:
                        S.add("pe", lambda e: e.matmul(psDen[:, 0:512], lhsT=onesf[:], rhs=acc[hs][:], start=True, stop=True),
                              [("acc", hs), "onesf"], ["psDen"])
                        S.add("dve", lambda e: e.reciprocal(out=recd[:], in_=psDen[:, 0:512]), ["psDen"], ["recd"])
                        S.add("dve", lambda e: e.tensor_tensor(out=mast[ms][:, h, :], in0=psOT[hs][:, 0:512], in1=recd[:],
                                                               op=ALU.mult), [("psOT", hs), "recd"], [("mast", ms, h)])
                        a2 = h % 2
                        S.add("act", lambda e: e.activation(out=asq[a2][:], in_=mast[ms][:, h, :], func=AF.Square),
                              [("mast", ms, h)], [("asq", a2)])

                        def mss(e):
                            ins = None
                            for j in range(4):
                                ins = e.matmul(psSSa[:, h * 4 + j:h * 4 + j + 1], lhsT=asq[a2][:, j * 128:(j + 1) * 128],
                                               rhs=onesb2[:, 0:1], start=True, stop=True)
                            return ins
                        S.add("pe", mss, [("asq", a2), "onesb2"], ["psSSa"])

                prev_h = [None]
                for (h, tiles) in steps:
                    if h != prev_h[0]:
                        hcnt[0] += 1
                        prev_h[0] = h
                        hs_new = hcnt[0] % 2
                        S.add("pool", lambda e, hs_new=hs_new: e.memset(acc[hs_new][:], 0.0), [], [("acc", hs_new), ("acc0", hs_new)])
                    hs = hcnt[0] % 2
                    sl = gstep[0] % 2
                    pl = gstep[0] % 3
                    gstep[0] += 1
                    kv = h // 2

                    def qk(e, sl=sl, kv=kv, h=h, tiles=tiles):
                        ins = None
                        for idx, kt in enumerate(tiles):
                            rows, c0 = kcols(kt)
                            ins = e.matmul(psS[sl][0:rows, idx * 512:(idx + 1) * 512], lhsT=KT[:, kv, c0:c0 + rows],
                                           rhs=QT[qs][:, h, :], start=True, stop=True)
                        return ins
                    S.add("pe", qk, [("QT", qs)], [("psS", sl)])
                    rows = kcols(tiles[0])[0]
                    wdt = 512 * len(tiles)
                    S.add("act", lambda e, sl=sl, pl=pl, rows=rows, wdt=wdt: e.activation(
                        out=pT[pl][0:rows, 0:wdt], in_=psS[sl][0:rows, 0:wdt], func=AF.Exp, scale=scale),
                        [("psS", sl)], [("pT", pl)])
                    if pend[0] is not None:
                        emit_pv(*pend[0])
                    pend[0] = (h, tiles, pl, hs)
                emit_pv(*pend[0])
                S.add("dve", lambda e: e.tensor_copy(out=sstmp[:], in_=psSSa[:, 0:16]), ["psSSa"], ["sstmp"])
                S.add("dve", lambda e: e.tensor_tensor(out=sstmp[:, 0:8], in0=sstmp[:, 0:8], in1=sstmp[:, 8:16], op=ALU.add),
                      ["sstmp"], ["sstmp"])
                S.add("dve", lambda e: e.tensor_tensor(out=ssa[:, qg * 4:(qg + 1) * 4], in0=sstmp[:, 0:4], in1=sstmp[:, 4:8],
                                                       op=ALU.add), ["sstmp"], [("ssa", qg * 4 + j) for j in range(4)])
                DMA(ma_dram[:, :, qg * 512:(qg + 1) * 512], mast[ms][:], [("mast", ms, h) for h in range(4)],
                    [("madram", qg)], ("mast", ms))
            for qg in range(nqg):
                do_qg(qg)
            if debug in ("B", "B_small"):
                DMA(dbg["st"][:, 0:nqg * 4], ssa[:, 0:nqg * 4], [("ssa", T) for T in range(nqg * 4)], ["dbgst"], "dbgst")
            S.emit()
        st_ab.close()
        if debug in ("B", "B_small"):
            S.final_fence()
            return nc

        with ExitStack() as st:
            w1 = sb(st, "w1", [N1, 2 * N1], BF16)
            dm = sb(st, "dm", [82, 2, N1 * MW], BF16)
            dc = sb(st, "dc", [128, 256], BF16)
            wf32 = sb(st, "wf32", [128, 4, 128], F32)
            wfb = sb(st, "wfb", [128, 4, 128], BF16)
            cw = sb(st, "cw", [64, 4, 2, 256], BF16)
            onesb = sb(st, "onesb", [128, 1], BF16)
            ublk = [sb(st, "ublk%d" % i, [N1, 32, N2], BF16) for i in range(2)]
            A_sb = sb(st, "A_sb", [82, 2, 2, N1, 64], BF16)
            PTQ = sb(st, "PTQ", [64, 2, 2, PTW], BF16)
            yst = [sb(st, "yst%d" % i, [128, 512], BF16) for i in range(2)]
            ysq = [sb(st, "ysq%d" % i, [128, 512], BF16) for i in range(2)]
            NW = 5
            psW = [ps(st, "psW%d" % i, [128, 512], F32) for i in range(NW)]
            ps1 = psW
            ps2 = psW
            psY = [ps(st, "psY%d" % i, [128, 512], F32) for i in range(2)]
            psC = psY[0]
            psSS = ps(st, "psSS", [128, 512], F32)
            DMA(w1[:], I["dftw1"], [], ["w1"], "w1")
            for ch in range(2):
                DMA(dm[:, ch, :], I["dftm"][ch], [], [("dm", ch)], ("dm", ch))
            DMA(dc[:], I["dftc"], [], ["dc"], "dc")
            DMA(wf32[:], I["w_four"].rearrange("g c d -> c g d"), [], ["wf32"], "wf32")
            S.add("dve", lambda e: e.tensor_copy(out=wfb[:], in_=wf32[:]), ["wf32"], ["wfb"])
            S.add("pool", lambda e: e.memset(onesb[:], 1.0), [], ["onesb"])
            for g in range(4):
                for hh in range(2):
                    def mmc(e, g=g, hh=hh):
                        e.matmul(psC[0:64, 0:128], lhsT=dc[:, 64 * hh:64 * hh + 64], rhs=wfb[:, g, :], start=True, stop=True)
                        return e.matmul(psC[0:64, 128:256], lhsT=dc[:, 128 + 64 * hh:128 + 64 * hh + 64], rhs=wfb[:, g, :],
                                        start=True, stop=True)
                    S.add("pe", mmc, ["dc", "wfb"], [("psY", 0)])
                    S.add("dve", lambda e, g=g, hh=hh: e.tensor_copy(out=cw[:, g, hh, 0:128], in_=psC[0:64, 0:128]),
                          [("psY", 0)], [("cw", g, hh, 0)])
                    S.add("dve", lambda e, g=g, hh=hh: e.tensor_scalar(out=cw[:, g, hh, 128:256], in0=psC[0:64, 128:256],
                                                                       scalar1=-1.0, scalar2=None, op0=ALU.mult),
                          [("psY", 0)], [("psY", 0), ("cw", g, hh, 1)])
            u3 = u_dram.rearrange("c (a b) -> a c b", b=N2)
            ev = 0
            p1 = 0
            p2 = 0
            ycnt = 0
            for hg in range(8):
                g, hh = hg // 2, hg % 2
                cbase = g * 128 + hh * 64
                for blk in range(2):
                    bs = (hg * 2 + blk) % 2
                    DMA(ublk[bs][:], u3[:, cbase + blk * 32:cbase + blk * 32 + 32, :],
                        [("udram", i) for i in range(NKT)] if (hg == 0 and blk == 0) else [], [("ublk", bs)], ("ublk", bs))
                    for pair in range(16):
                        for ch in range(2):
                            sl = p1 % NW
                            p1 += 1

                            def mm1(e, bs=bs, pair=pair, ch=ch, sl=sl):
                                ins = None
                                for ci in range(2):
                                    ins = e.matmul(ps1[sl][0:82, ci * 200:(ci + 1) * 200],
                                                   lhsT=ublk[bs][:, pair * 2 + ci, ch * 82:(ch + 1) * 82],
                                                   rhs=w1[:, :], start=True, stop=True)
                                return ins
                            S.add("pe", mm1, [("ublk", bs), "w1"], [("psW", sl)])
                            cl = blk * 32 + pair * 2
                            dst = _ap(A_sb[:, ch, 0, 0, cl:cl + 1], [[N1 * 64, 2], [64, N1], [1, 2]])
                            src = _ap(ps1[sl][0:82, 0:1], [[100, 2], [1, N1], [200, 2]])
                            if ev % 2 == 0:
                                S.add("act", lambda e, dst=dst, src=src: e.activation(out=dst, in_=src, func=AF.Copy),
                                      [("psW", sl)], [("A", hg, ch, cl)])
                            else:
                                S.add("dve", lambda e, dst=dst, src=src: e.tensor_copy(out=dst, in_=src),
                                      [("psW", sl)], [("A", hg, ch, cl)])
                            ev += 1
                akeys = [("A", hg, ch, cl) for ch in range(2) for cl in range(0, 64, 2)]
                k1 = 0
                while k1 < N1:
                    nq = min(6, N1 - k1)
                    sl = (p1 + p2) % NW
                    p2 += 1

                    def mm2(e, k1=k1, nq=nq, sl=sl):
                        ins = None
                        for q in range(nq):
                            kk = k1 + q
                            n = 0
                            for ch in range(2):
                                for ri in range(2):
                                    off = kk * MW + (T2 if ri == 0 else 0)
                                    ins = e.matmul(ps2[sl][0:64, q * 82:(q + 1) * 82], lhsT=A_sb[:, ch, ri, kk, :],
                                                   rhs=dm[:, ch, off:off + 82], start=(n == 0), stop=(n == 3))
                                    n += 1
                        return ins
                    S.add("pe", mm2, akeys + [("dm", 0), ("dm", 1)], [("psW", sl)])
                    for pq in range(2):
                        dst = _ap(PTQ[:, hh, pq, k1:k1 + 1], [[100, T2], [1, nq]])
                        src = _ap(ps2[sl][0:64, pq * T2:pq * T2 + 1], [[1, T2], [82, nq]])
                        eng = "act" if pq == 0 else "dve"
                        if eng == "act":
                            S.add("act", lambda e, dst=dst, src=src: e.activation(out=dst, in_=src, func=AF.Copy),
                                  [("psW", sl)], [("PTQ", hh, pq, k1)])
                        else:
                            S.add("dve", lambda e, dst=dst, src=src: e.tensor_copy(out=dst, in_=src),
                                  [("psW", sl)], [("PTQ", hh, pq, k1)])
                    k1 += nq
                if hh == 1:
                    pkeys = [("PTQ", a, b, c) for a in range(2) for b in range(2) for c in range(0, N1, 6)]
                    for tb in range(8):
                        ys = ycnt % 2
                        ycnt += 1

                        def mmy(e, g=g, tb=tb, ys=ys):
                            ins = None
                            n = 0
                            for h2 in range(2):
                                for pq in range(2):
                                    ins = e.matmul(psY[ys][:], lhsT=cw[:, g, h2, pq * 128:(pq + 1) * 128],
                                                   rhs=PTQ[:, h2, pq, tb * 512:(tb + 1) * 512], start=(n == 0), stop=(n == 3))
                                    n += 1
                            return ins
                        S.add("pe", mmy, pkeys + [("cw", g, a, b) for a in range(2) for b in range(2)], [("psY", ys)])
                        S.add("act", lambda e, ys=ys: e.activation(out=yst[ys][:], in_=psY[ys][:], func=AF.Copy),
                              [("psY", ys)], [("yst", ys)])
                        S.add("act", lambda e, ys=ys: e.activation(out=ysq[ys][:], in_=psY[ys][:], func=AF.Square),
                              [("psY", ys)], [("ysq", ys)])

                        def mms(e, ys=ys, tb=tb):
                            ins = None
                            for j in range(4):
                                T = tb * 4 + j
                                ins = e.matmul(psSS[:, T:T + 1], lhsT=ysq[ys][:, j * 128:(j + 1) * 128], rhs=onesb[:, 0:1],
                                               start=True, stop=True)
                            return ins
                        S.add("pe", mms, [("ysq", ys), "onesb"], ["psSS"])
                        DMA(mf_dram[:, g, tb * 512:(tb + 1) * 512], yst[ys][:], [("yst", ys)], [("mfdram", g, tb)], ("yst", ys))
                    S.add("dve", lambda e, g=g: e.tensor_copy(out=ssfp[:, g, :], in_=psSS[:, 0:32]), ["psSS"],
                          ["psSS", ("ssfp", g)])
            S.emit()
        if debug == "C":
            DMA(dbg["st"][:, 0:128], ssfp[:].rearrange("p a b -> p (a b)"), [], ["dbgst"], "dbgst")
            S.emit()
            S.final_fence()
            return nc

        st_d = top.enter_context(ExitStack())
        wo = sb(st_d, "wo", [128, 8, 1024], BF16)
        wu = sb(st_d, "wu", [128, 8, 4096], BF16)
        wd = sb(st_d, "wd", [128, 32, 1024], BF16)
        gfin_b = sb(st_d, "gfin_b", [128, 1024], F32)
        ra = sb(st_d, "ra", [128, 32], F32)
        rf = sb(st_d, "rf", [128, 32], F32)
        with ExitStack() as st:
            wst = [sb(st, "wst2_%d" % i, [128, 2048], F32) for i in range(3)]
            goT = sb(st, "goT", [128, 8], F32)
            gmlT = sb(st, "gmlT", [128, 8], F32)
            tmpa = sb(st, "tmpa", [128, 32], F32)
            tmpf = sb(st, "tmpf", [128, 32], F32)
            DMA(goT[:, 0:4], I["g_ao"].rearrange("(c p) -> p c", p=128), [], ["goTa"], "goTa", allow_slow_non_contiguous=True)
            DMA(goT[:, 4:8], I["g_fo"].rearrange("(c p) -> p c", p=128), [], ["goTf"], "goTf", allow_slow_non_contiguous=True)
            DMA(gmlT[:], I["g_mlp"].rearrange("(c p) -> p c", p=128), [], ["gmlT"], "gmlT", allow_slow_non_contiguous=True)
            DMA(gfin_b[:], I["g_final"].partition_broadcast(128), [], ["gfin_b"], "gfin_b")
            rms_rstd(ssa[:], tmpa[:], ra[:], 512.0, "ssa_all", "tmpa", "ra")
            S.add("dve", lambda e: e.tensor_tensor(out=tmpf[:], in0=ssfp[:, 0, :], in1=ssfp[:, 1, :], op=ALU.add), [], ["tf0"])
            S.add("dve", lambda e: e.tensor_tensor(out=tmpf[:], in0=tmpf[:], in1=ssfp[:, 2, :], op=ALU.add), ["tf0"], ["tf1"])
            S.add("dve", lambda e: e.tensor_tensor(out=tmpf[:], in0=tmpf[:], in1=ssfp[:, 3, :], op=ALU.add), ["tf1"], ["tf2"])
            rms_rstd(tmpf[:], tmpf[:], rf[:], 512.0, "tf2", "tf3", "rf")
            k = 0
            jobs = []
            for c in range(8):
                jobs.append((I["w_out"][c * 128:(c + 1) * 128, :], wo[:, c, :], goT[:, c:c + 1], 1024, ["goTa", "goTf"]))
            for c in range(8):
                for hf in range(2):
                    jobs.append((I["w_up"][c * 128:(c + 1) * 128, hf * 2048:(hf + 1) * 2048],
                                 wu[:, c, hf * 2048:(hf + 1) * 2048], gmlT[:, c:c + 1], 2048, ["gmlT"]))
            wd_v = I["w_down"].rearrange("(c p) n -> p c n", p=128)
            for c2 in range(16):
                jobs.append((wd_v[:, 2 * c2:2 * c2 + 2, :], wd[:, 2 * c2:2 * c2 + 2, :], None, 2048, []))
            for (src, dst, sc, n, rk) in jobs:
                s = k % 3
                if sc is None:
                    DMA(wst[s][:].rearrange("p (a b) -> p a b", a=2), src, [], [("wst2", s)], ("wst2", s))
                    dsto = dst.rearrange("p a b -> p (a b)")
                else:
                    DMA(wst[s][:, 0:n], src, [], [("wst2", s)], ("wst2", s))
                    dsto = dst
                eng = ("dve", "pool", "act")[k % 3]
                if sc is None:
                    if eng == "act":
                        S.add("act", lambda e, s=s, n=n, dsto=dsto: e.activation(out=dsto, in_=wst[s][:, 0:n], func=AF.Copy),
                              [("wst2", s)], [("wD", k)])
                    else:
                        S.add(eng, lambda e, s=s, n=n, dsto=dsto: e.tensor_copy(out=dsto, in_=wst[s][:, 0:n]),
                              [("wst2", s)], [("wD", k)])
                else:
                    if eng == "act":
                        S.add("act", lambda e, s=s, n=n, dsto=dsto, sc=sc: e.activation(out=dsto, in_=wst[s][:, 0:n],
                                                                                        func=AF.Copy, scale=sc),
                              [("wst2", s)] + rk, [("wD", k)])
                    else:
                        S.add(eng, lambda e, s=s, n=n, dsto=dsto, sc=sc: e.tensor_scalar(
                            out=dsto, in0=wst[s][:, 0:n], scalar1=sc, scalar2=None, op0=ALU.mult),
                            [("wst2", s)] + rk, [("wD", k)])
                k += 1
            S.emit()

        with ExitStack() as st:
            xg = [sb(st, "xg%d" % i, [128, 2, 1024], F32) for i in range(2)]
            mat = [sb(st, "mat%d" % i, [128, 4, 256], BF16) for i in range(2)]
            mft = [sb(st, "mft%d" % i, [128, 4, 256], BF16) for i in range(2)]
            mb = [sb(st, "mb%d" % i, [128, 1024], BF16) for i in range(2)]
            mT = [sb(st, "mT%d" % i, [128, 8, 256], BF16) for i in range(2)]
            aT = [sb(st, "aT%d" % i, [128, 256], BF16) for i in range(3)]
            aT2 = [sb(st, "aT2_%d" % i, [128, 256], BF16) for i in range(3)]
            stt = sb(st, "stt", [128, 8], F32)
            psAF = ps(st, "psAF", [128, 1024], F32)
            psT = psAF[:, 0:512].bitcast(BF16)
            psH = [ps(st, "psH%d" % i, [128, 512], F32) for i in range(2)]
            psD = [ps(st, "psD%d" % i, [128, 1024], F32) for i in range(2)]
            NG = 16 if debug != "D_small" else 2

            def P1_load(G):
                xs = G % 2
                DMA(xg[xs][:], I["xown"][G * 256:(G + 1) * 256, :].rearrange("(j p) f -> p j f", p=128), [],
                    [("xg", xs, 0), ("xg", xs, 1)], ("xg", xs))
                DMA(mat[xs][:], ma_dram[:, :, G * 256:(G + 1) * 256], [], [("mat", xs)], ("mat", xs))
                DMA(mft[xs][:], mf_dram[:, :, G * 256:(G + 1) * 256], [], [("mft", xs)], ("mft", xs))

            def P1_part(G, j, which):
                xs = G % 2
                T = 2 * G + j
                src, key, rr = ((mat, "mat", ra), (mft, "mft", rf))[which]

                def mmo(e):
                    ins = None
                    for hf in range(2):
                        for c in range(4):
                            ins = e.matmul(psAF[:, hf * 512:(hf + 1) * 512], lhsT=src[xs][:, c, j * 128:(j + 1) * 128],
                                           rhs=wo[:, which * 4 + c, hf * 512:(hf + 1) * 512], start=(c == 0), stop=(c == 3))
                    return ins
                S.add("pe", mmo, [(key, xs)], ["psAF"])
                S.add("dve", lambda e: e.scalar_tensor_tensor(
                    out=xg[xs][:, j, :], in0=psAF[:], scalar=rr[:, T:T + 1], in1=xg[xs][:, j, :],
                    op0=ALU.mult, op1=ALU.add), ["psAF", ("xg", xs, j)], ["psAF", ("xg", xs, j)])
                if which == 1:
                    ja, jk = jnk()
                    S.add("act", lambda e: e.activation(out=ja, in_=xg[xs][:, j, :], func=AF.Square,
                                                        accum_out=stt[:, j:j + 1]), [("xg", xs, j)], [("stt", j), jk])
                    rms_rstd(stt[:, j:j + 1], stt[:, 2 + j:3 + j], stt[:, 4 + j:5 + j], 1024.0, ("stt", j), ("stt", 2 + j), ("stt", 4 + j))
                    S.add("dve", lambda e: e.tensor_scalar(out=mb[j][:], in0=xg[xs][:, j, :], scalar1=stt[:, 4 + j:5 + j],
                                                           scalar2=None, op0=ALU.mult),
                          [("xg", xs, j), ("stt", 4 + j)], [("mb", j)])

            def P2_part(G, j):
                xs = G % 2

                def trm(e):
                    ins = None
                    for c in range(8):
                        ins = e.transpose(out=psT[:, c * 128:(c + 1) * 128], in_=mb[j][:, c * 128:(c + 1) * 128],
                                          identity=identb[:])
                    return ins
                S.add("pe", trm, [("mb", j)], ["psAF"])
                S.add("act", lambda e: e.activation(out=mT[xs][:, :, j * 128:(j + 1) * 128],
                                                    in_=psT.rearrange("p (a b) -> p a b", a=8), func=AF.Copy),
                      ["psAF"], [("mT", xs, j)])

            def P1(G):
                P1_load(G)
                for j in range(2):
                    P1_part(G, j, 0)
                    P1_part(G, j, 1)

            def P2(G):
                for j in range(2):
                    P2_part(G, j)

            SCHED = {2: lambda G: P1_load(G), 4: lambda G: P1_part(G, 0, 0), 8: lambda G: P1_part(G, 0, 1),
                     12: lambda G: P1_part(G, 1, 0), 16: lambda G: P1_part(G, 1, 1),
                     21: lambda G: P2_part(G, 0), 26: lambda G: P2_part(G, 1)}

            def down(G, f, sl):
                def mmd(e):
                    ins = None
                    for j in range(2):
                        for hf in range(2):
                            ins = e.matmul(psD[j][:, hf * 512:(hf + 1) * 512], lhsT=aT2[sl][:, j * 128:(j + 1) * 128],
                                           rhs=wd[:, f, hf * 512:(hf + 1) * 512], start=(f == 0), stop=(f == 31))
                    return ins
                S.add("pe", mmd, [("aT2", sl)], [("psD", 0), ("psD", 1)])

            P1(0)
            P2(0)
            fcnt = 0
            for G in range(NG):
                xs = G % 2
                pend = None
                for f in range(32):
                    hs = fcnt % 2
                    sl = fcnt % 3
                    fcnt += 1

                    def mmu2(e, f=f, hs=hs, xs=xs):
                        ins = None
                        for c in range(8):
                            ins = e.matmul(psH[hs][:, 0:256], lhsT=wu[:, c, f * 128:(f + 1) * 128],
                                           rhs=mT[xs][:, c, :], start=(c == 0), stop=(c == 7))
                        return ins
                    S.add("pe", mmu2, [("mT", xs, 0), ("mT", xs, 1)], [("psH", hs)])
                    S.add("act", lambda e, hs=hs, sl=sl: e.activation(out=aT[sl][:], in_=psH[hs][:, 0:256],
                                                                      func=AF.Relu), [("psH", hs)], [("aT", sl)])
                    S.add("pool", lambda e, sl=sl: e.tensor_tensor(out=aT2[sl][:], in0=aT[sl][:], in1=aT[sl][:], op=ALU.mult),
                          [("aT", sl)], [("aT2", sl)])
                    if pend is not None:
                        down(G, *pend)
                    pend = (f, sl)
                    if f in SCHED and G + 1 < NG:
                        SCHED[f](G + 1)
                down(G, *pend)
                for j in range(2):
                    S.add("dve", lambda e, j=j, xs=xs: e.tensor_tensor(out=xg[xs][:, j, :], in0=psD[j][:], in1=xg[xs][:, j, :], op=ALU.add),
                          [("psD", j), ("xg", xs, j)], [("psD", j), ("xg", xs, j)])
                for j in range(2):
                    ja, jk = jnk()
                    S.add("act", lambda e, j=j, xs=xs, ja=ja: e.activation(out=ja, in_=xg[xs][:, j, :], func=AF.Square,
                                                                           accum_out=stt[:, 6 + j:7 + j]),
                          [("xg", xs, j)], [("stt", 6 + j), jk])
                    rms_rstd(stt[:, 6 + j:7 + j], stt[:, 6 + j:7 + j], stt[:, 6 + j:7 + j], 1024.0,
                             ("stt", 6 + j), ("stt", 6 + j, "b"), ("stt", 6 + j, "c"))
                    S.add("dve", lambda e, j=j, xs=xs: e.scalar_tensor_tensor(
                        out=xg[xs][:, j, :], in0=xg[xs][:, j, :], scalar=stt[:, 6 + j:7 + j], in1=gfin_b[:],
                        op0=ALU.mult, op1=ALU.mult), [("xg", xs, j), ("stt", 6 + j, "c"), "gfin_b"], [("xg", xs, j)])
                DMA(out[G * 256:(G + 1) * 256, :].rearrange("(j p) f -> p j f", p=128), xg[xs][:],
                    [("xg", xs, 0), ("xg", xs, 1)], [("out", G)], ("xg", xs))
            S.emit()
        S.final_fence()
    return nc


def _rope_tables(rows, cols):
    inv = (np.float32(10000.0) ** (-np.arange(0, 64, 2, dtype=np.float32) / np.float32(64))).astype(np.float32)
    out = np.zeros((rows.shape[0], 256), np.float32)
    for part, pos in ((0, rows), (1, cols)):
        ang = (pos.astype(np.float32)[:, None] * inv[None, :]).astype(np.float32)
        c = np.cos(ang).astype(np.float32)
        s = np.sin(ang).astype(np.float32)
        b = part * 64
        out[:, b:b + 32] = c
        out[:, b + 32:b + 64] = c
        out[:, 128 + b:128 + b + 32] = -s
        out[:, 128 + b + 32:128 + b + 64] = s
    return out


def _positions():
    rows = np.concatenate([np.full(N_META, -1.0), np.repeat(np.arange(SEQ // 64), 64)]).astype(np.float32)
    cols = np.concatenate([np.arange(N_META), np.tile(np.arange(64), SEQ // 64)]).astype(np.float32)
    return rows, cols


def _dft_tables(j):
    bf = ml_dtypes.bfloat16
    k0 = N_META + OWN * j
    n1 = np.arange(N1)[:, None]
    k1p = np.arange(N1)[None, :]
    k1 = (k1p + k0) % N1
    ang = 2.0 * np.pi * ((n1 * k1) % N1) / N1
    w1 = np.concatenate([np.cos(ang), np.sin(ang)], axis=1).astype(bf)
    n2 = np.arange(N2, dtype=np.int64)[:, None, None]
    kk = (k0 + np.arange(N1, dtype=np.int64)[None, :, None] + N1 * np.arange(T2, dtype=np.int64)[None, None, :])
    th = 2.0 * np.pi * ((n2 * kk) % LTOT) / LTOT
    m = np.concatenate([-np.sin(th), np.cos(th), np.sin(th)], axis=2)
    m = m.reshape(2, 82, N1 * MW).astype(bf)
    return w1, m


def _dftc():
    c = np.arange(128)
    ang = 2.0 * np.pi * ((c[:, None] * c[None, :]) % 128) / 128.0
    sc = 1.0 / math.sqrt(128.0 * LTOT)
    return np.concatenate([np.cos(ang) * sc, np.sin(ang) * sc], axis=1).astype(ml_dtypes.bfloat16)


_CACHE = {}


def _in_maps(x, meta_tokens, g_mix, w_in, g_q, g_k, w_fourier, g_attn_out, g_fourier_out, w_out, g_mlp, w_up,
             w_down, g_final, cores):
    f = np.float32
    rows, cols = _positions()
    ropefull = _rope_tables(rows, cols)
    ropek = np.zeros((NKT * 128, 256), f)
    ropek[0:16] = ropefull[0:16]
    ropek[128:] = ropefull[16:]
    ident = np.eye(128, dtype=f).astype(ml_dtypes.bfloat16)
    dftc = _dftc()
    common = dict(
        meta=np.ascontiguousarray(meta_tokens, f), w_in=np.ascontiguousarray(w_in[0], f),
        w_out=np.ascontiguousarray(w_out[0], f), w_up=np.ascontiguousarray(w_up[0], f),
        w_down=np.ascontiguousarray(w_down[0], f), g_mix=np.ascontiguousarray(g_mix[0], f),
        g_q=np.ascontiguousarray(g_q[0], f), g_k=np.ascontiguousarray(g_k[0], f),
        g_ao=np.ascontiguousarray(g_attn_out[0], f), g_fo=np.ascontiguousarray(g_fourier_out[0], f),
        g_mlp=np.ascontiguousarray(g_mlp[0], f), g_final=np.ascontiguousarray(g_final, f),
        w_four=np.ascontiguousarray(w_fourier[0], f), ropek=ropek, ident=ident, dftc=dftc)
    maps = []
    for c in cores:
        b, j = c // 4, c % 4
        w1, m = _dft_tables(j)
        d = dict(common)
        d["xb"] = np.ascontiguousarray(x[b], f)
        d["xown"] = np.ascontiguousarray(x[b, OWN * j:OWN * (j + 1)], f)
        d["ropeq"] = np.ascontiguousarray(ropefull[16 + OWN * j:16 + OWN * (j + 1)])
        d["dftw1"] = w1
        d["dftm"] = m
        maps.append(d)
    return maps


def kernel(x, meta_tokens, g_mix, w_in, g_q, g_k, w_fourier, g_attn_out, g_fourier_out, w_out, g_mlp, w_up,
           w_down, g_final):
    cores = list(range(8))
    maps = _in_maps(x, meta_tokens, g_mix, w_in, g_q, g_k, w_fourier, g_attn_out, g_fourier_out, w_out, g_mlp,
                    w_up, w_down, g_final, cores)
    if "nc" not in _CACHE:
        _CACHE["nc"] = build_program()
    res = run_bass_kernel_spmd(_CACHE["nc"], maps, core_ids=cores)
    outp = np.zeros((2, SEQ, D_MODEL), np.float32)
    for c in cores:
        b, j = c // 4, c % 4
        outp[b, OWN * j:OWN * (j + 1)] = np.asarray(res.results[c]["out"], np.float32)
    return outp
```

```python
import math
import os
from contextlib import ExitStack

import numpy as np
import ml_dtypes

import concourse.bass as bass
import concourse.mybir as mybir
from concourse.bass_utils import run_bass_kernel_spmd

F32 = mybir.dt.float32
BF16 = mybir.dt.bfloat16
ALU = mybir.AluOpType
AF = mybir.ActivationFunctionType

D_MODEL = 1024
SEQ = 16384
N_META = 16
LTOT = SEQ + N_META
NKT = 129
OWN = 4096
EPS = 1e-6
N1 = 100
N2 = 164
T2 = 41
MW = 3 * T2
PTW = 4200


class _Op:
    __slots__ = ("eng", "fn", "deps", "is_dma", "semkey", "val", "signal")


class Sched:
    ENGS = ("pe", "act", "dve", "pool", "sp")
    BLK = {"pe": "tensor", "act": "scalar", "dve": "vector", "pool": "gpsimd", "sp": "sync"}

    def __init__(self, nc, stack):
        self.nc = nc
        self.stack = stack
        self.sems = {}
        self.count = {}
        self.waited = {e: {} for e in self.ENGS}
        self.prev = {}
        self._reset()

    def _reset(self):
        self.ops = {e: [] for e in self.ENGS}
        self.bufs = {}

    def _sem(self, key):
        if key not in self.sems:
            self.sems[key] = self.stack.enter_context(self.nc.semaphore("s%d" % len(self.sems)))
            self.count[key] = 0
        return self.sems[key]

    def add(self, eng, fn, reads=(), writes=(), dma_key=None):
        op = _Op()
        op.eng = eng
        op.fn = fn
        op.deps = []
        op.is_dma = dma_key is not None
        op.signal = op.is_dma
        op.val = None
        op.semkey = dma_key if op.is_dma else eng
        self._sem(op.semkey)
        if op.is_dma:
            self.count[dma_key] += 16
            op.val = self.count[dma_key]
        for b in reads:
            st = self.bufs.setdefault(b, [None, []])
            if st[0] is not None:
                op.deps.append((st[0], "raw"))
            name = b if isinstance(b, str) else b[0]
            if name.startswith("ps"):
                for r in st[1]:
                    if r.eng != eng:
                        op.deps.append((r, "xrr"))
            st[1].append(op)
        for b in writes:
            st = self.bufs.setdefault(b, [None, []])
            if st[0] is not None:
                op.deps.append((st[0], "waw"))
            for r in st[1]:
                if r is not op:
                    op.deps.append((r, "war"))
            st[0] = op
            st[1] = []
        self.ops[eng].append(op)
        return op

    @staticmethod
    def _needs_wait(op, d, kind):
        if d.is_dma:
            return True
        if d.eng == op.eng:
            return op.eng != "pe"
        return True

    def emit(self):
        nc = self.nc
        for e in self.ENGS:
            for op in self.ops[e]:
                for d, kind in op.deps:
                    if self._needs_wait(op, d, kind):
                        d.signal = True
            if e != "sp" and self.ops[e]:
                self.ops[e][-1].signal = True
        barrier = self.prev
        for e in self.ENGS:
            if e == "sp":
                continue
            self._sem(e)
            c = self.count[e]
            for op in self.ops[e]:
                if not op.is_dma and op.signal:
                    c += 1
                    op.val = c
            self.count[e] = c
        self.prev = dict(self.count)
        with nc.Block() as block:
            for e in self.ENGS:
                def body(eng, e=e):
                    waited = self.waited[e]
                    for key, v in barrier.items():
                        if key == e or v == 0:
                            continue
                        if waited.get(key, 0) < v:
                            eng.wait_ge(self.sems[key], v)
                            waited[key] = v
                    for op in self.ops[e]:
                        for d, kind in op.deps:
                            if not self._needs_wait(op, d, kind):
                                continue
                            if waited.get(d.semkey, 0) < d.val:
                                eng.wait_ge(self.sems[d.semkey], d.val)
                                waited[d.semkey] = d.val
                        ins = op.fn(eng)
                        if op.is_dma:
                            ins.then_inc(self.sems[op.semkey], 16)
                        elif op.signal:
                            ins.then_inc(self.sems[e], 1)
                getattr(block, self.BLK[e])(body)
        self._reset()

    def final_fence(self):
        nc = self.nc
        barrier = dict(self.count)
        with nc.Block() as block:
            for e in self.ENGS:
                def body(eng, e=e):
                    for key, v in barrier.items():
                        if key == e or v == 0:
                            continue
                        if self.waited[e].get(key, 0) < v:
                            eng.wait_ge(self.sems[key], v)
                            self.waited[e][key] = v
                getattr(block, self.BLK[e])(body)


def _ap(base, dims, extra_off=0):
    return bass.AP(base.tensor, base.offset + extra_off, [list(base.ap[0])] + [list(d) for d in dims])


def build_program(debug=None):
    nc = bass.Bass("TRN2", target_bir_lowering=False)
    I = {}

    def din(name, shape, dt):
        I[name] = nc.dram_tensor(name, list(shape), dt, kind="ExternalInput").ap()

    din("xb", [SEQ, 1024], F32)
    din("meta", [16, 1024], F32)
    din("xown", [OWN, 1024], F32)
    din("w_in", [1024, 1536], F32)
    din("w_out", [1024, 1024], F32)
    din("w_up", [1024, 4096], F32)
    din("w_down", [4096, 1024], F32)
    din("g_mix", [1024], F32)
    din("g_q", [128], F32)
    din("g_k", [128], F32)
    din("g_ao", [512], F32)
    din("g_fo", [512], F32)
    din("g_mlp", [1024], F32)
    din("g_final", [1024], F32)
    din("w_four", [4, 128, 128], F32)
    din("ropek", [NKT * 128, 256], F32)
    din("ropeq", [OWN, 256], F32)
    din("ident", [128, 128], BF16)
    din("dftw1", [N1, 2 * N1], BF16)
    din("dftm", [2, 82, N1 * MW], BF16)
    din("dftc", [128, 256], BF16)
    out = nc.dram_tensor("out", [OWN, 1024], F32, kind="ExternalOutput").ap()
    skind = "ExternalOutput" if debug else "Internal"
    u_dram = nc.dram_tensor("u_dram", [512, LTOT], BF16, kind=skind).ap()
    q_dram = nc.dram_tensor("q_dram", [128, 4, OWN], BF16, kind=skind).ap()
    ma_dram = nc.dram_tensor("ma_dram", [128, 4, OWN], BF16, kind=skind).ap()
    mf_dram = nc.dram_tensor("mf_dram", [128, 4, OWN], BF16, kind=skind).ap()
    wo_d = nc.dram_tensor("wo_d", [128, 8, 1024], BF16).ap()
    wu_d = nc.dram_tensor("wu_d", [128, 8, 4096], BF16).ap()
    wd_d = nc.dram_tensor("wd_d", [128, 32, 1024], BF16).ap()
    dbg = {}
    if debug:
        dbg["kt"] = nc.dram_tensor("dbg_kt", [128, 2 * LTOT], BF16, kind="ExternalOutput").ap()
        dbg["v"] = nc.dram_tensor("dbg_v", [128, NKT * 2 * 129], BF16, kind="ExternalOutput").ap()
        dbg["st"] = nc.dram_tensor("dbg_st", [128, 128], F32, kind="ExternalOutput").ap()

    with ExitStack() as top:
        S = Sched(nc, top)

        uniq = [0]

        def sb(stack, name, shape, dt):
            uniq[0] += 1
            return stack.enter_context(nc.sbuf_tensor("%s_%d" % (name, uniq[0]), list(shape), dt))

        def ps(stack, name, shape, dt):
            uniq[0] += 1
            return stack.enter_context(nc.psum_tensor("%s_%d" % (name, uniq[0]), list(shape), dt))

        def DMA(o, i, reads, writes, key, **kw):
            S.add("sp", lambda e: e.dma_start(out=o, in_=i, **kw), reads, writes, dma_key=key)

        identb = sb(top, "identb", [128, 128], BF16)
        epsc = sb(top, "epsc", [128, 1], F32)
        junk_t = sb(top, "junk", [128, 4, 1024], BF16)
        jstate = [0]

        def jnk(n=1024):
            k = jstate[0] % 4
            jstate[0] += 1
            return junk_t[:, k, 0:n], ("junk", k)
        ssa = sb(top, "ssa", [128, 32], F32)
        ssfp = sb(top, "ssfp", [128, 4, 32], F32)
        DMA(identb[:], I["ident"], [], ["identb"], "identb")
        goT = sb(top, "goT", [128, 8], F32)
        gmlT = sb(top, "gmlT", [128, 8], F32)
        DMA(goT[:, 0:4], I["g_ao"].rearrange("(c p) -> p c", p=128), [], ["goTa"], "goTa", allow_slow_non_contiguous=True)
        DMA(goT[:, 4:8], I["g_fo"].rearrange("(c p) -> p c", p=128), [], ["goTf"], "goTf", allow_slow_non_contiguous=True)
        DMA(gmlT[:], I["g_mlp"].rearrange("(c p) -> p c", p=128), [], ["gmlT"], "gmlT", allow_slow_non_contiguous=True)
        S.add("pool", lambda e: e.memset(epsc[:], EPS), [], ["epsc"])
        if debug:
            S.add("pool", lambda e: e.memset(ssa[:], 1.0), [], ["ssa_init"])

        def rms_rstd(ss_ap, sd_ap, rs_ap, n, kss, ksd, krs):
            S.add("act", lambda e: e.activation(out=sd_ap, in_=ss_ap, func=AF.Sqrt, bias=epsc[:, 0:1],
                                                scale=1.0 / n), (list(kss) if isinstance(kss, list) else [kss]) + ["epsc"], [ksd])
            S.add("dve", lambda e: e.reciprocal(out=rs_ap, in_=sd_ap), [ksd], [krs])

        st_ab = top.enter_context(ExitStack())
        KT = sb(st_ab, "KT", [128, 2, LTOT], BF16)
        VE = sb(st_ab, "VE", [128, NKT, 2, 129], BF16)
        st_a = top.enter_context(ExitStack())
        w_in_sb = sb(st_a, "w_in_sb", [128, 8, 1536], BF16)
        gq_b = sb(st_a, "gq_b", [128, 128], F32)
        gk_b = sb(st_a, "gk_b", [128, 128], F32)
        gmT = sb(st_a, "gmT", [128, 8], F32)

        with ExitStack() as st:
            wst = [sb(st, "wst%d" % i, [128, 768], F32) for i in range(2)]
            DMA(gmT[:], I["g_mix"].rearrange("(c p) -> p c", p=128), [], ["gmT"], "gmT",
                allow_slow_non_contiguous=True)
            DMA(gq_b[:], I["g_q"].partition_broadcast(128), [], ["gq_b"], "gq_b")
            DMA(gk_b[:], I["g_k"].partition_broadcast(128), [], ["gk_b"], "gk_b")
            S.add("pool", lambda e: e.memset(VE[:, :, :, 128:129], 1.0), [], ["VE1"])
            k = 0
            for c in range(8):
                for hf in range(2):
                    s = k % 2
                    DMA(wst[s][:], I["w_in"][c * 128:(c + 1) * 128, hf * 768:(hf + 1) * 768],
                        [], [("wst", s)], ("wst", s))
                    eng = "dve" if k % 2 == 0 else "pool"
                    S.add(eng, lambda e, s=s, c=c, hf=hf: e.tensor_scalar(
                        out=w_in_sb[:, c, hf * 768:(hf + 1) * 768], in0=wst[s][:],
                        scalar1=gmT[:, c:c + 1], scalar2=None, op0=ALU.mult),
                        [("wst", s), "gmT"], [("w_in_sb", c, hf)])
                    k += 1
            S.emit()

        if debug == "P0":
            DMA(dbg["kt"][:, 0:12288], w_in_sb[:].rearrange("p a b -> p (a b)"), [], ["dbgkt"], "dbgkt")
            DMA(dbg["st"], gq_b[:], [], ["dbgst"], "dbgst")
            S.emit()
            S.final_fence()
            return nc

        def alloc_front(st):
            fb = {}
            fb["xt"] = [sb(st, "xt%d" % i, [128, 1024], F32) for i in range(3)]
            fb["xn"] = [sb(st, "xn%d" % i, [128, 1024], BF16) for i in range(3)]
            fb["xnT"] = [sb(st, "xnT%d" % i, [128, 8, 128], BF16) for i in range(3)]
            fb["ss"] = [sb(st, "ss%d" % i, [128, 1], F32) for i in range(2)]
            fb["sd"] = [sb(st, "sd%d" % i, [128, 1], F32) for i in range(2)]
            fb["rs"] = [sb(st, "rs%d" % i, [128, 1], F32) for i in range(2)]
            fb["psT"] = [ps(st, "psT%d" % i, [128, 1024], BF16) for i in range(2)]
            return fb

        def front0(fb, i, src_rows=None, meta=False):
            x3 = i % 3
            xt = fb["xt"][x3]
            if meta:
                S.add("pool", lambda e: e.memset(xt[:], 0.0), [], [("xt", x3)])
                DMA(xt[0:16, :], I["meta"], [], [("xt", x3)], ("xt", x3))
            else:
                DMA(xt[:], src_rows, [], [("xt", x3)], ("xt", x3))

        def front1(fb, i):
            s, s3 = i % 2, i % 3
            ss, sd, rs = (fb[k][s] for k in ("ss", "sd", "rs"))
            xt = fb["xt"][s3]
            xn = fb["xn"][s3]
            ja, jk = jnk()
            S.add("act", lambda e: e.activation(out=ja, in_=xt[:], func=AF.Square, accum_out=ss[:, 0:1]),
                  [("xt", s3)], [("ss", s), jk])
            rms_rstd(ss[:, 0:1], sd[:, 0:1], rs[:, 0:1], 1024.0, ("ss", s), ("sd", s), ("rs", s))
            S.add("dve", lambda e: e.tensor_scalar(out=xn[:], in0=xt[:], scalar1=rs[:, 0:1], scalar2=None,
                                                   op0=ALU.mult), [("xt", s3), ("rs", s)], [("xn", s3)])

        def front2(fb, i):
            s, s3 = i % 2, i % 3
            xn, xnT, psT = fb["xn"][s3], fb["xnT"][s3], fb["psT"][s]

            def tr(e):
                ins = None
                for c in range(8):
                    ins = e.transpose(out=psT[:, c * 128:(c + 1) * 128], in_=xn[:, c * 128:(c + 1) * 128],
                                      identity=identb[:])
                return ins
            S.add("pe", tr, [("xn", s3), "identb"], [("psT", s)])
            S.add("act", lambda e: e.activation(out=xnT[:].rearrange("p a b -> p (a b)"), in_=psT[:], func=AF.Copy),
                  [("psT", s)], [("xnT", s3)])

        def head_norm_rope(nh, s, psrc, pkey, hb, gb, gkey, rp, tag, rps=0):
            hss, hsd, hrs, hn, t1, t2, kr = (hb[k][s] for k in ("hss", "hsd", "hrs", "hn", "t1", "t2", "kr"))
            W = nh * 128

            HV = os.environ.get("KHV", "psum")
            if HV == "copy":
                S.add("act", lambda e: e.activation(out=hn[:, 0:W], in_=psrc[:, 0:W], func=AF.Copy), [pkey], [(tag, "hraw", s)])
                sq_src, sq_key = hn, (tag, "hraw", s)
            else:
                sq_src, sq_key = psrc, pkey
            for h in range(nh):
                ja, jk = jnk(128)
                acc = hss[:, h:h + 1] if HV != "sep" else hb["hss1"][s][h][:, 0:1]
                S.add("act", lambda e, h=h, ja=ja, acc=acc: e.activation(out=ja, in_=sq_src[:, h * 128:(h + 1) * 128],
                                                                         func=AF.Square, accum_out=acc),
                      [sq_key], [(tag, "hss", s, h), jk])
            HS = os.environ.get("KSKIP", "").split(",")
            if "hsq" in HS:
                return
            rms_rstd(hss[:, 0:nh], hsd[:, 0:nh], hrs[:, 0:nh], 128.0, [(tag, "hss", s, h) for h in range(nh)],
                     (tag, "hsd", s), (tag, "hrs", s))
            if "hrms" in HS:
                return

            def nrm(e):
                ins = None
                for h in range(nh):
                    ins = e.scalar_tensor_tensor(out=hn[:, h * 128:(h + 1) * 128], in0=psrc[:, h * 128:(h + 1) * 128],
                                                 scalar=hrs[:, h:h + 1], in1=gb[:], op0=ALU.mult, op1=ALU.mult)
                return ins
            S.add("dve", nrm, [pkey, (tag, "hrs", s), gkey], [(tag, "hn", s)])
            if "hnrm" in HS:
                return
            RE = os.environ.get("KROPE", "pool")
            cosb = _ap(rp[:, 0:128], [[0, nh], [1, 128]])
            hn3 = _ap(hn[:, 0:W], [[128, nh], [1, 128]])
            t13 = _ap(t1[:, 0:W], [[128, nh], [1, 128]])
            S.add(RE, lambda e: e.tensor_tensor(out=t13, in0=hn3, in1=cosb, op=ALU.mult),
                  [(tag, "hn", s), (tag, "rp", rps)], [(tag, "t1", s)])

            def swp(e):
                ins = None
                for hf in range(2):
                    o = _ap(t2[:, 0:W], [[128, nh], [64, 2], [1, 32]], hf * 32)
                    a = _ap(hn[:, 0:W], [[128, nh], [64, 2], [1, 32]], (1 - hf) * 32)
                    b = _ap(rp[:, 128:256], [[0, nh], [64, 2], [1, 32]], hf * 32)
                    ins = e.tensor_tensor(out=o, in0=a, in1=b, op=ALU.mult)
                return ins
            S.add(RE, swp, [(tag, "hn", s), (tag, "rp", rps)], [(tag, "t2", s)])
            S.add(RE, lambda e: e.tensor_tensor(out=kr[:, 0:W], in0=t1[:, 0:W], in1=t2[:, 0:W], op=ALU.add),
                  [(tag, "t1", s), (tag, "t2", s)], [(tag, "kr", s)])

        def alloc_head(st, nh, tag):
            hb = {}
            W = nh * 128
            for nm, shp, dt in (("hss", [128, 4], F32), ("hsd", [128, 4], F32), ("hrs", [128, 4], F32),
                                ("hn", [128, W], F32), ("t1", [128, W], F32), ("t2", [128, W], F32),
                                ("kr", [128, W], BF16), ("rp", [128, 256], F32)):
                hb[nm] = [sb(st, "%s_%s%d" % (tag, nm, i), shp, dt) for i in range(4 if nm == "rp" else 2)]
            hb["hss1"] = [[sb(st, "%s_hss1_%d_%d" % (tag, i, h), [128, 1], F32) for h in range(nh)] for i in range(2)]
            return hb

        with ExitStack() as st:
            fb = alloc_front(st)
            hb = alloc_head(st, 2, "k")
            ust = [sb(st, "ust%d" % i, [128, 4, 128], BF16) for i in range(2)]
            psKV = [ps(st, "psKV%d" % i, [128, 512], F32) for i in range(2)]
            psU = [ps(st, "psU%d" % i, [128, 512], F32) for i in range(2)]
            psKT = [ps(st, "psKT%d" % i, [128, 1024], BF16) for i in range(2)]
            u_v = u_dram.rearrange("(j p) n -> p j n", p=128)
            ntile = NKT if debug != "A_small" else int(os.environ.get("KNT", "3"))
            SKIP = os.environ.get("KSKIP", "").split(",") if debug else []
            def tile_cols(i):
                return (16, 0) if i == 0 else (128, 16 + (i - 1) * 128)

            def A_F0(i):
                if i == 0:
                    front0(fb, i, meta=True)
                else:
                    front0(fb, i, src_rows=I["xb"][(i - 1) * 128:i * 128, :])
                r4 = i % 4
                DMA(hb["rp"][r4][:], I["ropek"][i * 128:(i + 1) * 128, :], [], [("k", "rp", r4)], ("k", "rp", r4))

            def A_B1(i):
                s, s3 = i % 2, i % 3
                n, c0 = tile_cols(i)
                xnT = fb["xnT"][s3]
                rp = hb["rp"][i % 4]

                def mmkv(e):
                    ins = None
                    for c in range(8):
                        ins = e.matmul(psKV[s][:], lhsT=xnT[:, c, :], rhs=w_in_sb[:, c, 512:1024],
                                       start=(c == 0), stop=(c == 7))
                    return ins
                S.add("pe", mmkv, [("xnT", s3)], [("psKV", s)])

                def mmu(e):
                    ins = None
                    for j in range(4):
                        for c in range(8):
                            ins = e.matmul(psU[s][:, j * 128:(j + 1) * 128],
                                           lhsT=w_in_sb[:, c, 1024 + j * 128:1024 + (j + 1) * 128],
                                           rhs=xnT[:, c, :], start=(c == 0), stop=(c == 7))
                    return ins
                S.add("pe", mmu, [("xnT", s3)], [("psU", s)])
                head_norm_rope(2, s, psKV[s], ("psKV", s), hb, gk_b, "gk_b", rp, "k", rps=i % 4)
                S.add("dve", lambda e: e.tensor_copy(
                    out=VE[:, i, :, 0:128], in_=psKV[s][:, 256:512].rearrange("p (a b) -> p a b", a=2)),
                    [("psKV", s)], [("VE", i)])
                S.add("dve", lambda e: e.tensor_copy(out=ust[s][:].rearrange("p a b -> p (a b)"), in_=psU[s][:]),
                      [("psU", s)], [("ust", s)])
                DMA(u_v[:, :, c0:c0 + n], ust[s][:, :, 0:n], [("ust", s)], [("udram", i)], ("ust", s))

            def A_B2(i):
                s = i % 2
                n, c0 = tile_cols(i)
                kr = hb["kr"][s]

                def trk(e):
                    ins = None
                    for h in range(2):
                        ins = e.transpose(out=psKT[s][:, h * 128:(h + 1) * 128], in_=kr[:, h * 128:(h + 1) * 128],
                                          identity=identb[:])
                    return ins
                S.add("pe", trk, [("k", "kr", s)], [("psKT", s)])
                S.add("dve", lambda e: e.tensor_copy(
                    out=KT[:, :, c0:c0 + n], in_=psKT[s][:, 0:256].rearrange("p (a b) -> p a b", a=2)[:, :, 0:n]),
                    [("psKT", s)], [("KT", i)])

            A_F0(0)
            for t in range(ntile + 3):
                if t + 1 < ntile:
                    A_F0(t + 1)
                if t < ntile:
                    front1(fb, t)
                if 0 <= t - 1 < ntile:
                    front2(fb, t - 1)
                if 0 <= t - 2 < ntile:
                    A_B1(t - 2)
                if 0 <= t - 3 < ntile:
                    A_B2(t - 3)
            if debug in ("A", "A_small"):
                ncol = 16 + (ntile - 1) * 128
                DMA(dbg["kt"].rearrange("p (a b) -> p a b", a=2)[:, :, 0:ncol], KT[:, :, 0:ncol],
                    [("KT", i) for i in range(ntile)], ["dbgkt"], "dbgkt")
                DMA(dbg["v"].rearrange("p (a b) -> p a b", a=NKT)[:, 0:ntile, :],
                    VE[:, 0:ntile, :, :].rearrange("p a b c -> p a (b c)"),
                    [("VE", i) for i in range(ntile)] + ["VE1"], ["dbgv"], "dbgv")
            S.emit()
        if debug in ("A", "A_small"):
            S.final_fence()
            return nc

        with ExitStack() as st:
            fb = alloc_front(st)
            hb = alloc_head(st, 4, "q")
            qst = [sb(st, "qst%d" % i, [128, 4, 128], BF16) for i in range(2)]
            psQ = [ps(st, "psQ%d" % i, [128, 512], F32) for i in range(2)]
            psQT = [ps(st, "psQT%d" % i, [128, 1024], BF16) for i in range(2)]
            def Q_F0(t):
                front0(fb, t, src_rows=I["xown"][t * 128:(t + 1) * 128, :])
                r4 = t % 4
                DMA(hb["rp"][r4][:], I["ropeq"][t * 128:(t + 1) * 128, :], [], [("q", "rp", r4)], ("q", "rp", r4))

            def Q_B1(t):
                s, s3 = t % 2, t % 3
                xnT = fb["xnT"][s3]

                def mmq(e):
                    ins = None
                    for c in range(8):
                        ins = e.matmul(psQ[s][:], lhsT=xnT[:, c, :], rhs=w_in_sb[:, c, 0:512],
                                       start=(c == 0), stop=(c == 7))
                    return ins
                S.add("pe", mmq, [("xnT", s3)], [("psQ", s)])
                head_norm_rope(4, s, psQ[s], ("psQ", s), hb, gq_b, "gq_b", hb["rp"][t % 4], "q", rps=t % 4)

            def Q_B2(t):
                s = t % 2
                kr = hb["kr"][s]

                def trq(e):
                    ins = None
                    for h in range(4):
                        ins = e.transpose(out=psQT[s][:, h * 128:(h + 1) * 128], in_=kr[:, h * 128:(h + 1) * 128],
                                          identity=identb[:])
                    return ins
                S.add("pe", trq, [("q", "kr", s)], [("psQT", s)])
                S.add("dve", lambda e: e.tensor_copy(out=qst[s][:].rearrange("p a b -> p (a b)"), in_=psQT[s][:, 0:512]),
                      [("psQT", s)], [("qst", s)])
                DMA(q_dram[:, :, t * 128:(t + 1) * 128], qst[s][:], [("qst", s)], [("qdram", t // 4)], ("qst", s))

            nq_t = 32 if debug != "C" else 0
            if nq_t:
                Q_F0(0)
            for t in range(nq_t + 3 if nq_t else 0):
                if t + 1 < nq_t:
                    Q_F0(t + 1)
                if t < nq_t:
                    front1(fb, t)
                if 0 <= t - 1 < nq_t:
                    front2(fb, t - 1)
                if 0 <= t - 2 < nq_t:
                    Q_B1(t - 2)
                if 0 <= t - 3 < nq_t:
                    Q_B2(t - 3)
            S.emit()
        st_a.close()
        if debug == "A2":
            S.final_fence()
            return nc

        with ExitStack() as st:
            QT = [sb(st, "QT%d" % i, [128, 4, 512], BF16) for i in range(2)]
            pT = [sb(st, "pT%d" % i, [128, 1024], BF16) for i in range(3)]
            attn = [sb(st, "attn%d" % i, [128, 4, 512], BF16) for i in range(2)]
            mast = [sb(st, "mast%d" % i, [128, 4, 512], BF16) for i in range(2)]
            rec = sb(st, "rec", [128, 4], F32)
            psS = [ps(st, "psS%d" % i, [128, 1024], F32) for i in range(2)]
            psO = [ps(st, "psO%d" % i, [128, 512], F32) for i in range(4)]
            psTa = psS[0][:, 0:512].bitcast(BF16)
            scale = 1.0 / math.sqrt(128.0)
            nqg = 8 if debug not in ("B_small", "D_small") else 1
            if debug == "C":
                nqg = 0

            def kcols(kt):
                return (16, 0) if kt == 0 else (128, 16 + (kt - 1) * 128)

            wstB = [sb(st, "wstB%d" % i, [128, 2048], F32) for i in range(2)]
            wbfB = [sb(st, "wbfB%d" % i, [128, 2048], BF16) for i in range(2)]
            wjobs = []
            for c in range(8):
                wjobs.append((I["w_out"][c * 128:(c + 1) * 128, :], wo_d[:, c, :], goT[:, c:c + 1], 1024, 1))
            for c in range(8):
                for hf in range(2):
                    wjobs.append((I["w_up"][c * 128:(c + 1) * 128, hf * 2048:(hf + 1) * 2048],
                                  wu_d[:, c, hf * 2048:(hf + 1) * 2048], gmlT[:, c:c + 1], 2048, 1))
            wd_v = I["w_down"].rearrange("(c p) n -> p c n", p=128)
            for c2 in range(16):
                wjobs.append((wd_v[:, 2 * c2:2 * c2 + 2, :], wd_d[:, 2 * c2:2 * c2 + 2, :], None, 2048, 2))
            wjob_i = [0]

            def emit_wjob():
                k = wjob_i[0]
                if k >= len(wjobs):
                    return
                wjob_i[0] += 1
                wsrc, wdst, sc, n, a = wjobs[k]
                s = k % 2
                if a == 2:
                    DMA(wstB[s][:].rearrange("p (a b) -> p a b", a=2), wsrc, [], [("wstB", s)], ("wstB", s))
                    S.add("dve", lambda e: e.tensor_copy(out=wbfB[s][:], in_=wstB[s][:]), [("wstB", s)], [("wbfB", s)])
                    DMA(wdst, wbfB[s][:].rearrange("p (a b) -> p a b", a=2), [("wbfB", s)], [("wscr", k)], ("wbfB", s))
                else:
                    DMA(wstB[s][:, 0:n], wsrc, [], [("wstB", s)], ("wstB", s))
                    S.add("dve", lambda e: e.tensor_scalar(out=wbfB[s][:, 0:n], in0=wstB[s][:, 0:n], scalar1=sc, scalar2=None,
                                                           op0=ALU.mult), [("wstB", s)], [("wbfB", s)])
                    DMA(wdst, wbfB[s][:, 0:n], [("wbfB", s)], [("wscr", k)], ("wbfB", s))

            def do_qg(qg):
                qs = qg % 2
                DMA(QT[qs][:], q_dram[:, :, qg * 512:(qg + 1) * 512], [("qdram", qg)], [("QT", qs)], ("QT", qs))
                steps = [(h, tiles) for h in range(4) for tiles in ([[0]] + [[a, a + 1] for a in range(1, NKT, 2)])]
                cnt = 0
                pend = None

                def emit_pv(h, tiles, sl, pl):
                    kv = h // 2

                    def pv(e):
                        ins = None
                        for idx, kt in enumerate(tiles):
                            rows, _ = kcols(kt)
                            for j in range(4):
                                ins = e.matmul(psO[j][:, 0:129],
                                               lhsT=pT[pl][0:rows, idx * 512 + j * 128:idx * 512 + (j + 1) * 128],
                                               rhs=VE[0:rows, kt, kv, :], start=(kt == 0), stop=(kt == NKT - 1))
                        return ins
                    S.add("pe", pv, [("pT", pl)], [("psO", j) for j in range(4)])
                    if tiles[-1] == NKT - 1:
                        S.add("dve", lambda e: [e.reciprocal(out=rec[:, j:j + 1], in_=psO[j][:, 128:129])
                                                for j in range(4)][-1],
                              [("psO", j) for j in range(4)], ["rec"])
                        S.add("dve", lambda e: [e.tensor_scalar(out=attn[qs][:, j, h * 128:(h + 1) * 128],
                                                                in0=psO[j][:, 0:128], scalar1=rec[:, j:j + 1],
                                                                scalar2=None, op0=ALU.mult) for j in range(4)][-1],
                              [("psO", j) for j in range(4)] + ["rec"], [("psO", j) for j in range(4)] + [("attn", qs, h)])

                for (h, tiles) in steps:
                    sl = cnt % 2
                    pl = cnt % 3
                    kv = h // 2

                    def qk(e, sl=sl, kv=kv, h=h, tiles=tiles):
                        ins = None
                        for idx, kt in enumerate(tiles):
                            rows, c0 = kcols(kt)
                            ins = e.matmul(psS[sl][0:rows, idx * 512:(idx + 1) * 512], lhsT=KT[:, kv, c0:c0 + rows],
                                           rhs=QT[qs][:, h, :], start=True, stop=True)
                        return ins
                    S.add("pe", qk, [("QT", qs)], [("psS", sl)])
                    rows = kcols(tiles[0])[0]
                    wdt = 512 * len(tiles)
                    S.add("act", lambda e, sl=sl, pl=pl, rows=rows, wdt=wdt: e.activation(
                        out=pT[pl][0:rows, 0:wdt], in_=psS[sl][0:rows, 0:wdt], func=AF.Exp, scale=scale),
                        [("psS", sl)], [("pT", pl)])
                    if pend is not None:
                        emit_pv(*pend)
                    pend = (h, tiles, sl, pl)
                    cnt += 1
                    if cnt % 50 == 25:
                        emit_wjob()
                emit_pv(*pend)
                ms = qg % 2
                for j in range(4):
                    T = qg * 4 + j
                    ja, jk = jnk(512)
                    S.add("act", lambda e, j=j, T=T, ja=ja: e.activation(out=ja, in_=attn[qs][:, j, :],
                                                                         func=AF.Square, accum_out=ssa[:, T:T + 1]),
                          [("attn", qs, h) for h in range(4)], [("ssa", T), jk])

                    def tra(e, j=j):
                        ins = None
                        for h in range(4):
                            ins = e.transpose(out=psTa[:, h * 128:(h + 1) * 128], in_=attn[qs][:, j, h * 128:(h + 1) * 128],
                                              identity=identb[:])
                        return ins
                    S.add("pe", tra, [("attn", qs, h) for h in range(4)], [("psS", 0)])
                    S.add("dve", lambda e, j=j: e.tensor_copy(
                        out=mast[ms][:, :, j * 128:(j + 1) * 128],
                        in_=psTa[:, 0:512].rearrange("p (a b) -> p a b", a=4)), [("psS", 0)], [("mast", ms, j)])
                DMA(ma_dram[:, :, qg * 512:(qg + 1) * 512], mast[ms][:], [("mast", ms, j) for j in range(4)],
                    [("madram", qg)], ("mast", ms))
            for qg in range(nqg):
                do_qg(qg)
            while wjob_i[0] < len(wjobs):
                emit_wjob()
            if debug in ("B", "B_small"):
                DMA(dbg["st"][:, 0:nqg * 4], ssa[:, 0:nqg * 4], [("ssa", T) for T in range(nqg * 4)], ["dbgst"], "dbgst")
            S.emit()
        st_ab.close()
        if debug in ("B", "B_small"):
            S.final_fence()
            return nc

        with ExitStack() as st:
            w1 = sb(st, "w1", [N1, 2 * N1], BF16)
            dm = sb(st, "dm", [82, 2, N1 * MW], BF16)
            dc = sb(st, "dc", [128, 256], BF16)
            wf32 = sb(st, "wf32", [128, 4, 128], F32)
            wfb = sb(st, "wfb", [128, 4, 128], BF16)
            cw = sb(st, "cw", [64, 4, 2, 256], BF16)
            onesb = sb(st, "onesb", [128, 1], BF16)
            ublk = [sb(st, "ublk%d" % i, [N1, 32, N2], BF16) for i in range(2)]
            A_sb = sb(st, "A_sb", [82, 2, 2, N1, 64], BF16)
            PTQ = sb(st, "PTQ", [64, 2, 2, PTW], BF16)
            yst = [sb(st, "yst%d" % i, [128, 512], BF16) for i in range(2)]
            ysq = [sb(st, "ysq%d" % i, [128, 512], BF16) for i in range(2)]
            NW = 5
            psW = [ps(st, "psW%d" % i, [128, 512], F32) for i in range(NW)]
            ps1 = psW
            ps2 = psW
            psY = [ps(st, "psY%d" % i, [128, 512], F32) for i in range(2)]
            psC = psY[0]
            psSS = ps(st, "psSS", [128, 512], F32)
            DMA(w1[:], I["dftw1"], [], ["w1"], "w1")
            for ch in range(2):
                DMA(dm[:, ch, :], I["dftm"][ch], [], [("dm", ch)], ("dm", ch))
            DMA(dc[:], I["dftc"], [], ["dc"], "dc")
            DMA(wf32[:], I["w_four"].rearrange("g c d -> c g d"), [], ["wf32"], "wf32")
            S.add("dve", lambda e: e.tensor_copy(out=wfb[:], in_=wf32[:]), ["wf32"], ["wfb"])
            S.add("pool", lambda e: e.memset(onesb[:], 1.0), [], ["onesb"])
            for g in range(4):
                for hh in range(2):
                    def mmc(e, g=g, hh=hh):
                        e.matmul(psC[0:64, 0:128], lhsT=dc[:, 64 * hh:64 * hh + 64], rhs=wfb[:, g, :], start=True, stop=True)
                        return e.matmul(psC[0:64, 128:256], lhsT=dc[:, 128 + 64 * hh:128 + 64 * hh + 64], rhs=wfb[:, g, :],
                                        start=True, stop=True)
                    S.add("pe", mmc, ["dc", "wfb"], [("psY", 0)])
                    S.add("dve", lambda e, g=g, hh=hh: e.tensor_copy(out=cw[:, g, hh, 0:128], in_=psC[0:64, 0:128]),
                          [("psY", 0)], [("cw", g, hh, 0)])
                    S.add("dve", lambda e, g=g, hh=hh: e.tensor_scalar(out=cw[:, g, hh, 128:256], in0=psC[0:64, 128:256],
                                                                       scalar1=-1.0, scalar2=None, op0=ALU.mult),
                          [("psY", 0)], [("psY", 0), ("cw", g, hh, 1)])
            u3 = u_dram.rearrange("c (a b) -> a c b", b=N2)
            ev = 0
            p1 = 0
            p2 = 0
            ycnt = 0
            for hg in range(8):
                g, hh = hg // 2, hg % 2
                cbase = g * 128 + hh * 64
                for blk in range(2):
                    bs = (hg * 2 + blk) % 2
                    DMA(ublk[bs][:], u3[:, cbase + blk * 32:cbase + blk * 32 + 32, :],
                        [("udram", i) for i in range(NKT)] if (hg == 0 and blk == 0) else [], [("ublk", bs)], ("ublk", bs))
                    for pair in range(16):
                        for ch in range(2):
                            sl = p1 % NW
                            p1 += 1

                            def mm1(e, bs=bs, pair=pair, ch=ch, sl=sl):
                                ins = None
                                for ci in range(2):
                                    ins = e.matmul(ps1[sl][0:82, ci * 200:(ci + 1) * 200],
                                                   lhsT=ublk[bs][:, pair * 2 + ci, ch * 82:(ch + 1) * 82],
                                                   rhs=w1[:, :], start=True, stop=True)
                                return ins
                            S.add("pe", mm1, [("ublk", bs), "w1"], [("psW", sl)])
                            cl = blk * 32 + pair * 2
                            dst = _ap(A_sb[:, ch, 0, 0, cl:cl + 1], [[N1 * 64, 2], [64, N1], [1, 2]])
                            src = _ap(ps1[sl][0:82, 0:1], [[100, 2], [1, N1], [200, 2]])
                            if ev % 2 == 0:
                                S.add("act", lambda e, dst=dst, src=src: e.activation(out=dst, in_=src, func=AF.Copy),
                                      [("psW", sl)], [("A", hg, ch, cl)])
                            else:
                                S.add("dve", lambda e, dst=dst, src=src: e.tensor_copy(out=dst, in_=src),
                                      [("psW", sl)], [("A", hg, ch, cl)])
                            ev += 1
                akeys = [("A", hg, ch, cl) for ch in range(2) for cl in range(0, 64, 2)]
                k1 = 0
                while k1 < N1:
                    nq = min(6, N1 - k1)
                    sl = (p1 + p2) % NW
                    p2 += 1

                    def mm2(e, k1=k1, nq=nq, sl=sl):
                        ins = None
                        for q in range(nq):
                            kk = k1 + q
                            n = 0
                            for ch in range(2):
                                for ri in range(2):
                                    off = kk * MW + (T2 if ri == 0 else 0)
                                    ins = e.matmul(ps2[sl][0:64, q * 82:(q + 1) * 82], lhsT=A_sb[:, ch, ri, kk, :],
                                                   rhs=dm[:, ch, off:off + 82], start=(n == 0), stop=(n == 3))
                                    n += 1
                        return ins
                    S.add("pe", mm2, akeys + [("dm", 0), ("dm", 1)], [("psW", sl)])
                    for pq in range(2):
                        dst = _ap(PTQ[:, hh, pq, k1:k1 + 1], [[100, T2], [1, nq]])
                        src = _ap(ps2[sl][0:64, pq * T2:pq * T2 + 1], [[1, T2], [82, nq]])
                        eng = "act" if pq == 0 else "dve"
                        if eng == "act":
                            S.add("act", lambda e, dst=dst, src=src: e.activation(out=dst, in_=src, func=AF.Copy),
                                  [("psW", sl)], [("PTQ", hh, pq, k1)])
                        else:
                            S.add("dve", lambda e, dst=dst, src=src: e.tensor_copy(out=dst, in_=src),
                                  [("psW", sl)], [("PTQ", hh, pq, k1)])
                    k1 += nq
                if hh == 1:
                    pkeys = [("PTQ", a, b, c) for a in range(2) for b in range(2) for c in range(0, N1, 6)]
                    for tb in range(8):
                        ys = ycnt % 2
                        ycnt += 1

                        def mmy(e, g=g, tb=tb, ys=ys):
                            ins = None
                            n = 0
                            for h2 in range(2):
                                for pq in range(2):
                                    ins = e.matmul(psY[ys][:], lhsT=cw[:, g, h2, pq * 128:(pq + 1) * 128],
                                                   rhs=PTQ[:, h2, pq, tb * 512:(tb + 1) * 512], start=(n == 0), stop=(n == 3))
                                    n += 1
                            return ins
                        S.add("pe", mmy, pkeys + [("cw", g, a, b) for a in range(2) for b in range(2)], [("psY", ys)])
                        S.add("act", lambda e, ys=ys: e.activation(out=yst[ys][:], in_=psY[ys][:], func=AF.Copy),
                              [("psY", ys)], [("yst", ys)])
                        S.add("act", lambda e, ys=ys: e.activation(out=ysq[ys][:], in_=psY[ys][:], func=AF.Square),
                              [("psY", ys)], [("ysq", ys)])

                        def mms(e, ys=ys, tb=tb):
                            ins = None
                            for j in range(4):
                                T = tb * 4 + j
                                ins = e.matmul(psSS[:, T:T + 1], lhsT=ysq[ys][:, j * 128:(j + 1) * 128], rhs=onesb[:, 0:1],
                                               start=True, stop=True)
                            return ins
                        S.add("pe", mms, [("ysq", ys), "onesb"], ["psSS"])
                        DMA(mf_dram[:, g, tb * 512:(tb + 1) * 512], yst[ys][:], [("yst", ys)], [("mfdram", g, tb)], ("yst", ys))
                    S.add("dve", lambda e, g=g: e.tensor_copy(out=ssfp[:, g, :], in_=psSS[:, 0:32]), ["psSS"],
                          ["psSS", ("ssfp", g)])
            S.emit()
        if debug == "C":
            DMA(dbg["st"][:, 0:128], ssfp[:].rearrange("p a b -> p (a b)"), [], ["dbgst"], "dbgst")
            S.emit()
            S.final_fence()
            return nc

        st_d = top.enter_context(ExitStack())
        wo = sb(st_d, "wo", [128, 8, 1024], BF16)
        wu = sb(st_d, "wu", [128, 8, 4096], BF16)
        wd = sb(st_d, "wd", [128, 32, 1024], BF16)
        gfin_b = sb(st_d, "gfin_b", [128, 1024], F32)
        ra = sb(st_d, "ra", [128, 32], F32)
        rf = sb(st_d, "rf", [128, 32], F32)
        with ExitStack() as st:
            tmpa = sb(st, "tmpa", [128, 32], F32)
            tmpf = sb(st, "tmpf", [128, 32], F32)
            DMA(gfin_b[:], I["g_final"].partition_broadcast(128), [], ["gfin_b"], "gfin_b")
            rms_rstd(ssa[:], tmpa[:], ra[:], 512.0, "ssa_all", "tmpa", "ra")
            S.add("dve", lambda e: e.tensor_tensor(out=tmpf[:], in0=ssfp[:, 0, :], in1=ssfp[:, 1, :], op=ALU.add), [], ["tf0"])
            S.add("dve", lambda e: e.tensor_tensor(out=tmpf[:], in0=tmpf[:], in1=ssfp[:, 2, :], op=ALU.add), ["tf0"], ["tf1"])
            S.add("dve", lambda e: e.tensor_tensor(out=tmpf[:], in0=tmpf[:], in1=ssfp[:, 3, :], op=ALU.add), ["tf1"], ["tf2"])
            rms_rstd(tmpf[:], tmpf[:], rf[:], 512.0, "tf2", "tf3", "rf")
            for c in range(0, 8, 4):
                DMA(wo[:, c:c + 4, :], wo_d[:, c:c + 4, :], [], [("wo", c)], ("wo", c))
            for c in range(8):
                DMA(wu[:, c, :], wu_d[:, c, :], [], [("wu", c)], ("wu", c))
            for c in range(0, 32, 4):
                DMA(wd[:, c:c + 4, :], wd_d[:, c:c + 4, :], [], [("wd", c)], ("wd", c))
            S.emit()

        with ExitStack() as st:
            xg = [sb(st, "xg%d" % i, [128, 2, 1024], F32) for i in range(2)]
            mat = [sb(st, "mat%d" % i, [128, 4, 256], BF16) for i in range(2)]
            mft = [sb(st, "mft%d" % i, [128, 4, 256], BF16) for i in range(2)]
            mb = [sb(st, "mb%d" % i, [128, 1024], BF16) for i in range(2)]
            mT = [sb(st, "mT%d" % i, [128, 8, 256], BF16) for i in range(2)]
            aT = [sb(st, "aT%d" % i, [128, 256], BF16) for i in range(3)]
            aT2 = [sb(st, "aT2_%d" % i, [128, 256], BF16) for i in range(3)]
            stt = sb(st, "stt", [128, 8], F32)
            psAF = ps(st, "psAF", [128, 1024], F32)
            psT = psAF[:, 0:512].bitcast(BF16)
            psH = [ps(st, "psH%d" % i, [128, 512], F32) for i in range(2)]
            psD = [ps(st, "psD%d" % i, [128, 1024], F32) for i in range(2)]
            NG = 16 if debug != "D_small" else 2

            def P1_load(G):
                xs = G % 2
                DMA(xg[xs][:], I["xown"][G * 256:(G + 1) * 256, :].rearrange("(j p) f -> p j f", p=128), [],
                    [("xg", xs, 0), ("xg", xs, 1)], ("xg", xs))
                DMA(mat[xs][:], ma_dram[:, :, G * 256:(G + 1) * 256], [], [("mat", xs)], ("mat", xs))
                DMA(mft[xs][:], mf_dram[:, :, G * 256:(G + 1) * 256], [], [("mft", xs)], ("mft", xs))

            def P1_part(G, j, which):
                xs = G % 2
                T = 2 * G + j
                src, key, rr = ((mat, "mat", ra), (mft, "mft", rf))[which]

                def mmo(e):
                    ins = None
                    for hf in range(2):
                        for c in range(4):
                            ins = e.matmul(psAF[:, hf * 512:(hf + 1) * 512], lhsT=src[xs][:, c, j * 128:(j + 1) * 128],
                                           rhs=wo[:, which * 4 + c, hf * 512:(hf + 1) * 512], start=(c == 0), stop=(c == 3))
                    return ins
                S.add("pe", mmo, [(key, xs)], ["psAF"])
                S.add("dve", lambda e: e.scalar_tensor_tensor(
                    out=xg[xs][:, j, :], in0=psAF[:], scalar=rr[:, T:T + 1], in1=xg[xs][:, j, :],
                    op0=ALU.mult, op1=ALU.add), ["psAF", ("xg", xs, j)], ["psAF", ("xg", xs, j)])
                if which == 1:
                    ja, jk = jnk()
                    S.add("act", lambda e: e.activation(out=ja, in_=xg[xs][:, j, :], func=AF.Square,
                                                        accum_out=stt[:, j:j + 1]), [("xg", xs, j)], [("stt", j), jk])
                    rms_rstd(stt[:, j:j + 1], stt[:, 2 + j:3 + j], stt[:, 4 + j:5 + j], 1024.0, ("stt", j), ("stt", 2 + j), ("stt", 4 + j))
                    S.add("dve", lambda e: e.tensor_scalar(out=mb[j][:], in0=xg[xs][:, j, :], scalar1=stt[:, 4 + j:5 + j],
                                                           scalar2=None, op0=ALU.mult),
                          [("xg", xs, j), ("stt", 4 + j)], [("mb", j)])

            def P2_part(G, j):
                xs = G % 2

                def trm(e):
                    ins = None
                    for c in range(8):
                        ins = e.transpose(out=psT[:, c * 128:(c + 1) * 128], in_=mb[j][:, c * 128:(c + 1) * 128],
                                          identity=identb[:])
                    return ins
                S.add("pe", trm, [("mb", j)], ["psAF"])
                S.add("act", lambda e: e.activation(out=mT[xs][:, :, j * 128:(j + 1) * 128],
                                                    in_=psT.rearrange("p (a b) -> p a b", a=8), func=AF.Copy),
                      ["psAF"], [("mT", xs, j)])

            def P1(G):
                P1_load(G)
                for j in range(2):
                    P1_part(G, j, 0)
                    P1_part(G, j, 1)

            def P2(G):
                for j in range(2):
                    P2_part(G, j)

            SCHED = {2: lambda G: P1_load(G), 4: lambda G: P1_part(G, 0, 0), 8: lambda G: P1_part(G, 0, 1),
                     12: lambda G: P1_part(G, 1, 0), 16: lambda G: P1_part(G, 1, 1),
                     21: lambda G: P2_part(G, 0), 26: lambda G: P2_part(G, 1)}

            def down(G, f, sl):
                def mmd(e):
                    ins = None
                    for j in range(2):
                        for hf in range(2):
                            ins = e.matmul(psD[j][:, hf * 512:(hf + 1) * 512], lhsT=aT2[sl][:, j * 128:(j + 1) * 128],
                                           rhs=wd[:, f, hf * 512:(hf + 1) * 512], start=(f == 0), stop=(f == 31))
                    return ins
                S.add("pe", mmd, [("aT2", sl)], [("psD", 0), ("psD", 1)])

            P1(0)
            P2(0)
            fcnt = 0
            for G in range(NG):
                xs = G % 2
                pend = None
                for f in range(32):
                    hs = fcnt % 2
                    sl = fcnt % 3
                    fcnt += 1

                    def mmu2(e, f=f, hs=hs, xs=xs):
                        ins = None
                        for c in range(8):
                            ins = e.matmul(psH[hs][:, 0:256], lhsT=wu[:, c, f * 128:(f + 1) * 128],
                                           rhs=mT[xs][:, c, :], start=(c == 0), stop=(c == 7))
                        return ins
                    S.add("pe", mmu2, [("mT", xs, 0), ("mT", xs, 1)], [("psH", hs)])
                    S.add("act", lambda e, hs=hs, sl=sl: e.activation(out=aT[sl][:], in_=psH[hs][:, 0:256],
                                                                      func=AF.Relu), [("psH", hs)], [("aT", sl)])
                    S.add("pool", lambda e, sl=sl: e.tensor_tensor(out=aT2[sl][:], in0=aT[sl][:], in1=aT[sl][:], op=ALU.mult),
                          [("aT", sl)], [("aT2", sl)])
                    if pend is not None:
                        down(G, *pend)
                    pend = (f, sl)
                    if f in SCHED and G + 1 < NG:
                        SCHED[f](G + 1)
                down(G, *pend)
                for j in range(2):
                    S.add("dve", lambda e, j=j, xs=xs: e.tensor_tensor(out=xg[xs][:, j, :], in0=psD[j][:], in1=xg[xs][:, j, :], op=ALU.add),
                          [("psD", j), ("xg", xs, j)], [("psD", j), ("xg", xs, j)])
                for j in range(2):
                    ja, jk = jnk()
                    S.add("act", lambda e, j=j, xs=xs, ja=ja: e.activation(out=ja, in_=xg[xs][:, j, :], func=AF.Square,
                                                                           accum_out=stt[:, 6 + j:7 + j]),
                          [("xg", xs, j)], [("stt", 6 + j), jk])
                    rms_rstd(stt[:, 6 + j:7 + j], stt[:, 6 + j:7 + j], stt[:, 6 + j:7 + j], 1024.0,
                             ("stt", 6 + j), ("stt", 6 + j, "b"), ("stt", 6 + j, "c"))
                    S.add("dve", lambda e, j=j, xs=xs: e.scalar_tensor_tensor(
                        out=xg[xs][:, j, :], in0=xg[xs][:, j, :], scalar=stt[:, 6 + j:7 + j], in1=gfin_b[:],
                        op0=ALU.mult, op1=ALU.mult), [("xg", xs, j), ("stt", 6 + j, "c"), "gfin_b"], [("xg", xs, j)])
                DMA(out[G * 256:(G + 1) * 256, :].rearrange("(j p) f -> p j f", p=128), xg[xs][:],
                    [("xg", xs, 0), ("xg", xs, 1)], [("out", G)], ("xg", xs))
            S.emit()
        S.final_fence()
    return nc


def _rope_tables(rows, cols):
    inv = (np.float32(10000.0) ** (-np.arange(0, 64, 2, dtype=np.float32) / np.float32(64))).astype(np.float32)
    out = np.zeros((rows.shape[0], 256), np.float32)
    for part, pos in ((0, rows), (1, cols)):
        ang = (pos.astype(np.float32)[:, None] * inv[None, :]).astype(np.float32)
        c = np.cos(ang).astype(np.float32)
        s = np.sin(ang).astype(np.float32)
        b = part * 64
        out[:, b:b + 32] = c
        out[:, b + 32:b + 64] = c
        out[:, 128 + b:128 + b + 32] = -s
        out[:, 128 + b + 32:128 + b + 64] = s
    return out


def _positions():
    rows = np.concatenate([np.full(N_META, -1.0), np.repeat(np.arange(SEQ // 64), 64)]).astype(np.float32)
    cols = np.concatenate([np.arange(N_META), np.tile(np.arange(64), SEQ // 64)]).astype(np.float32)
    return rows, cols


def _dft_tables(j):
    bf = ml_dtypes.bfloat16
    k0 = N_META + OWN * j
    n1 = np.arange(N1)[:, None]
    k1p = np.arange(N1)[None, :]
    k1 = (k1p + k0) % N1
    ang = 2.0 * np.pi * ((n1 * k1) % N1) / N1
    w1 = np.concatenate([np.cos(ang), np.sin(ang)], axis=1).astype(bf)
    n2 = np.arange(N2, dtype=np.int64)[:, None, None]
    kk = (k0 + np.arange(N1, dtype=np.int64)[None, :, None] + N1 * np.arange(T2, dtype=np.int64)[None, None, :])
    th = 2.0 * np.pi * ((n2 * kk) % LTOT) / LTOT
    m = np.concatenate([-np.sin(th), np.cos(th), np.sin(th)], axis=2)
    m = m.reshape(2, 82, N1 * MW).astype(bf)
    return w1, m


def _dftc():
    c = np.arange(128)
    ang = 2.0 * np.pi * ((c[:, None] * c[None, :]) % 128) / 128.0
    sc = 1.0 / math.sqrt(128.0 * LTOT)
    return np.concatenate([np.cos(ang) * sc, np.sin(ang) * sc], axis=1).astype(ml_dtypes.bfloat16)


_CACHE = {}


def _in_maps(x, meta_tokens, g_mix, w_in, g_q, g_k, w_fourier, g_attn_out, g_fourier_out, w_out, g_mlp, w_up,
             w_down, g_final, cores):
    f = np.float32
    rows, cols = _positions()
    ropefull = _rope_tables(rows, cols)
    ropek = np.zeros((NKT * 128, 256), f)
    ropek[0:16] = ropefull[0:16]
    ropek[128:] = ropefull[16:]
    ident = np.eye(128, dtype=f).astype(ml_dtypes.bfloat16)
    dftc = _dftc()
    common = dict(
        meta=np.ascontiguousarray(meta_tokens, f), w_in=np.ascontiguousarray(w_in[0], f),
        w_out=np.ascontiguousarray(w_out[0], f), w_up=np.ascontiguousarray(w_up[0], f),
        w_down=np.ascontiguousarray(w_down[0], f), g_mix=np.ascontiguousarray(g_mix[0], f),
        g_q=np.ascontiguousarray(g_q[0], f), g_k=np.ascontiguousarray(g_k[0], f),
        g_ao=np.ascontiguousarray(g_attn_out[0], f), g_fo=np.ascontiguousarray(g_fourier_out[0], f),
        g_mlp=np.ascontiguousarray(g_mlp[0], f), g_final=np.ascontiguousarray(g_final, f),
        w_four=np.ascontiguousarray(w_fourier[0], f), ropek=ropek, ident=ident, dftc=dftc)
    maps = []
    for c in cores:
        b, j = c // 4, c % 4
        w1, m = _dft_tables(j)
        d = dict(common)
        d["xb"] = np.ascontiguousarray(x[b], f)
        d["xown"] = np.ascontiguousarray(x[b, OWN * j:OWN * (j + 1)], f)
        d["ropeq"] = np.ascontiguousarray(ropefull[16 + OWN * j:16 + OWN * (j + 1)])
        d["dftw1"] = w1
        d["dftm"] = m
        maps.append(d)
    return maps


def kernel(x, meta_tokens, g_mix, w_in, g_q, g_k, w_fourier, g_attn_out, g_fourier_out, w_out, g_mlp, w_up,
           w_down, g_final):
    cores = list(range(8))
    maps = _in_maps(x, meta_tokens, g_mix, w_in, g_q, g_k, w_fourier, g_attn_out, g_fourier_out, w_out, g_mlp,
                    w_up, w_down, g_final, cores)
    if "nc" not in _CACHE:
        _CACHE["nc"] = build_program()
    res = run_bass_kernel_spmd(_CACHE["nc"], maps, core_ids=cores)
    outp = np.zeros((2, SEQ, D_MODEL), np.float32)
    for c in cores:
        b, j = c // 4, c % 4
        outp[b, OWN * j:OWN * (j + 1)] = np.asarray(res.results[c]["out"], np.float32)
    return outp
```

```python
import math
import os
from contextlib import ExitStack

import numpy as np
import ml_dtypes

import concourse.bass as bass
import concourse.mybir as mybir
from concourse.bass_utils import run_bass_kernel_spmd

F32 = mybir.dt.float32
BF16 = mybir.dt.bfloat16
ALU = mybir.AluOpType
AF = mybir.ActivationFunctionType

D_MODEL = 1024
SEQ = 16384
N_META = 16
LTOT = SEQ + N_META
NKT = 129
OWN = 4096
EPS = 1e-6
N1 = 100
N2 = 164
T2 = 41
MW = 3 * T2
PTW = 4200


class _Op:
    __slots__ = ("eng", "fn", "deps", "is_dma", "semkey", "val", "signal")


class Sched:
    ENGS = ("pe", "act", "dve", "pool", "sp")
    BLK = {"pe": "tensor", "act": "scalar", "dve": "vector", "pool": "gpsimd", "sp": "sync"}

    def __init__(self, nc, stack):
        self.nc = nc
        self.stack = stack
        self.sems = {}
        self.count = {}
        self.waited = {e: {} for e in self.ENGS}
        self.prev = {}
        self._reset()

    def _reset(self):
        self.ops = {e: [] for e in self.ENGS}
        self.bufs = {}

    def _sem(self, key):
        if key not in self.sems:
            self.sems[key] = self.stack.enter_context(self.nc.semaphore("s%d" % len(self.sems)))
            self.count[key] = 0
        return self.sems[key]

    def add(self, eng, fn, reads=(), writes=(), dma_key=None):
        op = _Op()
        op.eng = eng
        op.fn = fn
        op.deps = []
        op.is_dma = dma_key is not None
        op.signal = op.is_dma
        op.val = None
        op.semkey = dma_key if op.is_dma else eng
        self._sem(op.semkey)
        if op.is_dma:
            self.count[dma_key] += 16
            op.val = self.count[dma_key]
        for b in reads:
            st = self.bufs.setdefault(b, [None, []])
            if st[0] is not None:
                op.deps.append((st[0], "raw"))
            name = b if isinstance(b, str) else b[0]
            if name.startswith("ps"):
                for r in st[1]:
                    if r.eng != eng:
                        op.deps.append((r, "xrr"))
            st[1].append(op)
        for b in writes:
            st = self.bufs.setdefault(b, [None, []])
            if st[0] is not None:
                op.deps.append((st[0], "waw"))
            for r in st[1]:
                if r is not op:
                    op.deps.append((r, "war"))
            st[0] = op
            st[1] = []
        self.ops[eng].append(op)
        return op

    @staticmethod
    def _needs_wait(op, d, kind):
        if d.is_dma:
            return True
        if d.eng == op.eng:
            return op.eng != "pe"
        return True

    def emit(self):
        nc = self.nc
        for e in self.ENGS:
            for op in self.ops[e]:
                for d, kind in op.deps:
                    if self._needs_wait(op, d, kind):
                        d.signal = True
            if e != "sp" and self.ops[e]:
                self.ops[e][-1].signal = True
        barrier = self.prev
        for e in self.ENGS:
            if e == "sp":
                continue
            self._sem(e)
            c = self.count[e]
            for op in self.ops[e]:
                if not op.is_dma and op.signal:
                    c += 1
                    op.val = c
            self.count[e] = c
        self.prev = dict(self.count)
        with nc.Block() as block:
            for e in self.ENGS:
                def body(eng, e=e):
                    waited = self.waited[e]
                    for key, v in barrier.items():
                        if key == e or v == 0:
                            continue
                        if waited.get(key, 0) < v:
                            eng.wait_ge(self.sems[key], v)
                            waited[key] = v
                    for op in self.ops[e]:
                        for d, kind in op.deps:
                            if not self._needs_wait(op, d, kind):
                                continue
                            if waited.get(d.semkey, 0) < d.val:
                                eng.wait_ge(self.sems[d.semkey], d.val)
                                waited[d.semkey] = d.val
                        ins = op.fn(eng)
                        if op.is_dma:
                            ins.then_inc(self.sems[op.semkey], 16)
                        elif op.signal:
                            ins.then_inc(self.sems[e], 1)
                getattr(block, self.BLK[e])(body)
        self._reset()

    def final_fence(self):
        nc = self.nc
        barrier = dict(self.count)
        with nc.Block() as block:
            for e in self.ENGS:
                def body(eng, e=e):
                    for key, v in barrier.items():
                        if key == e or v == 0:
                            continue
                        if self.waited[e].get(key, 0) < v:
                            eng.wait_ge(self.sems[key], v)
                            self.waited[e][key] = v
                getattr(block, self.BLK[e])(body)


def _ap(base, dims, extra_off=0):
    return bass.AP(base.tensor, base.offset + extra_off, [list(base.ap[0])] + [list(d) for d in dims])


def build_program(debug=None):
    nc = bass.Bass("TRN2", target_bir_lowering=False)
    I = {}

    def din(name, shape, dt):
        I[name] = nc.dram_tensor(name, list(shape), dt, kind="ExternalInput").ap()

    din("xb", [SEQ, 1024], F32)
    din("meta", [16, 1024], F32)
    din("xown", [OWN, 1024], F32)
    din("w_in", [1024, 1536], F32)
    din("w_out", [1024, 1024], F32)
    din("w_up", [1024, 4096], F32)
    din("w_down", [4096, 1024], F32)
    din("g_mix", [1024], F32)
    din("g_q", [128], F32)
    din("g_k", [128], F32)
    din("g_ao", [512], F32)
    din("g_fo", [512], F32)
    din("g_mlp", [1024], F32)
    din("g_final", [1024], F32)
    din("w_four", [4, 128, 128], F32)
    din("ropek", [NKT * 128, 256], F32)
    din("ropeq", [OWN, 256], F32)
    din("ident", [128, 128], BF16)
    din("dftw1", [N1, 2 * N1], BF16)
    din("dftm", [2, 82, N1 * MW], BF16)
    din("dftc", [128, 256], BF16)
    out = nc.dram_tensor("out", [OWN, 1024], F32, kind="ExternalOutput").ap()
    skind = "ExternalOutput" if debug else "Internal"
    u_dram = nc.dram_tensor("u_dram", [512, LTOT], BF16, kind=skind).ap()
    q_dram = nc.dram_tensor("q_dram", [128, 4, OWN], BF16, kind=skind).ap()
    ma_dram = nc.dram_tensor("ma_dram", [128, 4, OWN], BF16, kind=skind).ap()
    mf_dram = nc.dram_tensor("mf_dram", [128, 4, OWN], BF16, kind=skind).ap()
    wo_d = nc.dram_tensor("wo_d", [128, 8, 1024], BF16).ap()
    wu_d = nc.dram_tensor("wu_d", [128, 8, 4096], BF16).ap()
    wd_d = nc.dram_tensor("wd_d", [128, 32, 1024], BF16).ap()
    dbg = {}
    if debug:
        dbg["kt"] = nc.dram_tensor("dbg_kt", [128, 2 * LTOT], BF16, kind="ExternalOutput").ap()
        dbg["v"] = nc.dram_tensor("dbg_v", [128, NKT * 2 * 129], BF16, kind="ExternalOutput").ap()
        dbg["st"] = nc.dram_tensor("dbg_st", [128, 128], F32, kind="ExternalOutput").ap()

    with ExitStack() as top:
        S = Sched(nc, top)

        uniq = [0]

        def sb(stack, name, shape, dt):
            uniq[0] += 1
            return stack.enter_context(nc.sbuf_tensor("%s_%d" % (name, uniq[0]), list(shape), dt))

        def ps(stack, name, shape, dt):
            uniq[0] += 1
            return stack.enter_context(nc.psum_tensor("%s_%d" % (name, uniq[0]), list(shape), dt))

        def DMA(o, i, reads, writes, key, **kw):
            S.add("sp", lambda e: e.dma_start(out=o, in_=i, **kw), reads, writes, dma_key=key)

        identb = sb(top, "identb", [128, 128], BF16)
        epsc = sb(top, "epsc", [128, 1], F32)
        junk_t = sb(top, "junk", [128, 4, 1024], BF16)
        jstate = [0]

        def jnk(n=1024):
            k = jstate[0] % 4
            jstate[0] += 1
            return junk_t[:, k, 0:n], ("junk", k)
        ssa = sb(top, "ssa", [128, 32], F32)
        ssfp = sb(top, "ssfp", [128, 4, 32], F32)
        DMA(identb[:], I["ident"], [], ["identb"], "identb")
        goT = sb(top, "goT", [128, 8], F32)
        gmlT = sb(top, "gmlT", [128, 8], F32)
        DMA(goT[:, 0:4], I["g_ao"].rearrange("(c p) -> p c", p=128), [], ["goTa"], "goTa", allow_slow_non_contiguous=True)
        DMA(goT[:, 4:8], I["g_fo"].rearrange("(c p) -> p c", p=128), [], ["goTf"], "goTf", allow_slow_non_contiguous=True)
        DMA(gmlT[:], I["g_mlp"].rearrange("(c p) -> p c", p=128), [], ["gmlT"], "gmlT", allow_slow_non_contiguous=True)
        S.add("pool", lambda e: e.memset(epsc[:], EPS), [], ["epsc"])
        if debug:
            S.add("pool", lambda e: e.memset(ssa[:], 1.0), [], ["ssa_init"])

        def rms_rstd(ss_ap, sd_ap, rs_ap, n, kss, ksd, krs):
            S.add("act", lambda e: e.activation(out=sd_ap, in_=ss_ap, func=AF.Sqrt, bias=epsc[:, 0:1],
                                                scale=1.0 / n), (list(kss) if isinstance(kss, list) else [kss]) + ["epsc"], [ksd])
            S.add("dve", lambda e: e.reciprocal(out=rs_ap, in_=sd_ap), [ksd], [krs])

        st_ab = top.enter_context(ExitStack())
        KT = sb(st_ab, "KT", [128, 2, LTOT], BF16)
        VE = sb(st_ab, "VE", [128, NKT, 2, 129], BF16)
        st_a = top.enter_context(ExitStack())
        w_in_sb = sb(st_a, "w_in_sb", [128, 8, 1536], BF16)
        gq_b = sb(st_a, "gq_b", [128, 128], F32)
        gk_b = sb(st_a, "gk_b", [128, 128], F32)
        gmT = sb(st_a, "gmT", [128, 8], F32)

        with ExitStack() as st:
            wst = [sb(st, "wst%d" % i, [128, 768], F32) for i in range(2)]
            DMA(gmT[:], I["g_mix"].rearrange("(c p) -> p c", p=128), [], ["gmT"], "gmT",
                allow_slow_non_contiguous=True)
            DMA(gq_b[:], I["g_q"].partition_broadcast(128), [], ["gq_b"], "gq_b")
            DMA(gk_b[:], I["g_k"].partition_broadcast(128), [], ["gk_b"], "gk_b")
            S.add("pool", lambda e: e.memset(VE[:, :, :, 128:129], 1.0), [], ["VE1"])
            k = 0
            for c in range(8):
                for hf in range(2):
                    s = k % 2
                    DMA(wst[s][:], I["w_in"][c * 128:(c + 1) * 128, hf * 768:(hf + 1) * 768],
                        [], [("wst", s)], ("wst", s))
                    eng = "dve" if k % 2 == 0 else "pool"
                    S.add(eng, lambda e, s=s, c=c, hf=hf: e.tensor_scalar(
                        out=w_in_sb[:, c, hf * 768:(hf + 1) * 768], in0=wst[s][:],
                        scalar1=gmT[:, c:c + 1], scalar2=None, op0=ALU.mult),
                        [("wst", s), "gmT"], [("w_in_sb", c, hf)])
                    k += 1
            S.emit()

        if debug == "P0":
            DMA(dbg["kt"][:, 0:12288], w_in_sb[:].rearrange("p a b -> p (a b)"), [], ["dbgkt"], "dbgkt")
            DMA(dbg["st"], gq_b[:], [], ["dbgst"], "dbgst")
            S.emit()
            S.final_fence()
            return nc

        def alloc_front(st):
            fb = {}
            fb["xt"] = [sb(st, "xt%d" % i, [128, 1024], F32) for i in range(3)]
            fb["xn"] = [sb(st, "xn%d" % i, [128, 1024], BF16) for i in range(3)]
            fb["xnT"] = [sb(st, "xnT%d" % i, [128, 8, 128], BF16) for i in range(3)]
            fb["ss"] = [sb(st, "ss%d" % i, [128, 1], F32) for i in range(2)]
            fb["sd"] = [sb(st, "sd%d" % i, [128, 1], F32) for i in range(2)]
            fb["rs"] = [sb(st, "rs%d" % i, [128, 1], F32) for i in range(2)]
            fb["psT"] = [ps(st, "psT%d" % i, [128, 1024], BF16) for i in range(2)]
            return fb

        def front0(fb, i, src_rows=None, meta=False):
            x3 = i % 3
            xt = fb["xt"][x3]
            if meta:
                S.add("pool", lambda e: e.memset(xt[:], 0.0), [], [("xt", x3)])
                DMA(xt[0:16, :], I["meta"], [], [("xt", x3)], ("xt", x3))
            else:
                DMA(xt[:], src_rows, [], [("xt", x3)], ("xt", x3))

        def front1(fb, i):
            s, s3 = i % 2, i % 3
            ss, sd, rs = (fb[k][s] for k in ("ss", "sd", "rs"))
            xt = fb["xt"][s3]
            xn = fb["xn"][s3]
            ja, jk = jnk()
            S.add("act", lambda e: e.activation(out=ja, in_=xt[:], func=AF.Square, accum_out=ss[:, 0:1]),
                  [("xt", s3)], [("ss", s), jk])
            rms_rstd(ss[:, 0:1], sd[:, 0:1], rs[:, 0:1], 1024.0, ("ss", s), ("sd", s), ("rs", s))
            S.add("dve", lambda e: e.tensor_scalar(out=xn[:], in0=xt[:], scalar1=rs[:, 0:1], scalar2=None,
                                                   op0=ALU.mult), [("xt", s3), ("rs", s)], [("xn", s3)])

        def front2(fb, i):
            s, s3 = i % 2, i % 3
            xn, xnT, psT = fb["xn"][s3], fb["xnT"][s3], fb["psT"][s]

            def tr(e):
                ins = None
                for c in range(8):
                    ins = e.transpose(out=psT[:, c * 128:(c + 1) * 128], in_=xn[:, c * 128:(c + 1) * 128],
                                      identity=identb[:])
                return ins
            S.add("pe", tr, [("xn", s3), "identb"], [("psT", s)])
            S.add("act", lambda e: e.activation(out=xnT[:].rearrange("p a b -> p (a b)"), in_=psT[:], func=AF.Copy),
                  [("psT", s)], [("xnT", s3)])

        def head_norm_rope(nh, s, psrc, pkey, hb, gb, gkey, rp, tag, rps=0):
            hss, hsd, hrs, hn, t1, t2, kr = (hb[k][s] for k in ("hss", "hsd", "hrs", "hn", "t1", "t2", "kr"))
            W = nh * 128

            HV = os.environ.get("KHV", "psum")
            if HV == "copy":
                S.add("act", lambda e: e.activation(out=hn[:, 0:W], in_=psrc[:, 0:W], func=AF.Copy), [pkey], [(tag, "hraw", s)])
                sq_src, sq_key = hn, (tag, "hraw", s)
            else:
                sq_src, sq_key = psrc, pkey
            for h in range(nh):
                ja, jk = jnk(128)
                acc = hss[:, h:h + 1] if HV != "sep" else hb["hss1"][s][h][:, 0:1]
                S.add("act", lambda e, h=h, ja=ja, acc=acc: e.activation(out=ja, in_=sq_src[:, h * 128:(h + 1) * 128],
                                                                         func=AF.Square, accum_out=acc),
                      [sq_key], [(tag, "hss", s, h), jk])
            HS = os.environ.get("KSKIP", "").split(",")
            if "hsq" in HS:
                return
            rms_rstd(hss[:, 0:nh], hsd[:, 0:nh], hrs[:, 0:nh], 128.0, [(tag, "hss", s, h) for h in range(nh)],
                     (tag, "hsd", s), (tag, "hrs", s))
            if "hrms" in HS:
                return

            def nrm(e):
                ins = None
                for h in range(nh):
                    ins = e.scalar_tensor_tensor(out=hn[:, h * 128:(h + 1) * 128], in0=psrc[:, h * 128:(h + 1) * 128],
                                                 scalar=hrs[:, h:h + 1], in1=gb[:], op0=ALU.mult, op1=ALU.mult)
                return ins
            S.add("dve", nrm, [pkey, (tag, "hrs", s), gkey], [(tag, "hn", s)])
            if "hnrm" in HS:
                return
            RE = os.environ.get("KROPE", "pool")
            cosb = _ap(rp[:, 0:128], [[0, nh], [1, 128]])
            hn3 = _ap(hn[:, 0:W], [[128, nh], [1, 128]])
            t13 = _ap(t1[:, 0:W], [[128, nh], [1, 128]])
            S.add(RE, lambda e: e.tensor_tensor(out=t13, in0=hn3, in1=cosb, op=ALU.mult),
                  [(tag, "hn", s), (tag, "rp", rps)], [(tag, "t1", s)])

            def swp(e):
                ins = None
                for hf in range(2):
                    o = _ap(t2[:, 0:W], [[128, nh], [64, 2], [1, 32]], hf * 32)
                    a = _ap(hn[:, 0:W], [[128, nh], [64, 2], [1, 32]], (1 - hf) * 32)
                    b = _ap(rp[:, 128:256], [[0, nh], [64, 2], [1, 32]], hf * 32)
                    ins = e.tensor_tensor(out=o, in0=a, in1=b, op=ALU.mult)
                return ins
            S.add(RE, swp, [(tag, "hn", s), (tag, "rp", rps)], [(tag, "t2", s)])
            S.add(RE, lambda e: e.tensor_tensor(out=kr[:, 0:W], in0=t1[:, 0:W], in1=t2[:, 0:W], op=ALU.add),
                  [(tag, "t1", s), (tag, "t2", s)], [(tag, "kr", s)])

        def alloc_head(st, nh, tag):
            hb = {}
            W = nh * 128
            for nm, shp, dt in (("hss", [128, 4], F32), ("hsd", [128, 4], F32), ("hrs", [128, 4], F32),
                                ("hn", [128, W], F32), ("t1", [128, W], F32), ("t2", [128, W], F32),
                                ("kr", [128, W], BF16), ("rp", [128, 256], F32)):
                hb[nm] = [sb(st, "%s_%s%d" % (tag, nm, i), shp, dt) for i in range(4 if nm == "rp" else 2)]
            hb["hss1"] = [[sb(st, "%s_hss1_%d_%d" % (tag, i, h), [128, 1], F32) for h in range(nh)] for i in range(2)]
            return hb

        with ExitStack() as st:
            fb = alloc_front(st)
            hb = alloc_head(st, 2, "k")
            ust = [sb(st, "ust%d" % i, [128, 4, 128], BF16) for i in range(2)]
            psKV = [ps(st, "psKV%d" % i, [128, 512], F32) for i in range(2)]
            psU = [ps(st, "psU%d" % i, [128, 512], F32) for i in range(2)]
            psKT = [ps(st, "psKT%d" % i, [128, 1024], BF16) for i in range(2)]
            u_v = u_dram.rearrange("(j p) n -> p j n", p=128)
            ntile = NKT if debug != "A_small" else int(os.environ.get("KNT", "3"))
            SKIP = os.environ.get("KSKIP", "").split(",") if debug else []
            def tile_cols(i):
                return (16, 0) if i == 0 else (128, 16 + (i - 1) * 128)

            def A_F0(i):
                if i == 0:
                    front0(fb, i, meta=True)
                else:
                    front0(fb, i, src_rows=I["xb"][(i - 1) * 128:i * 128, :])
                r4 = i % 4
                DMA(hb["rp"][r4][:], I["ropek"][i * 128:(i + 1) * 128, :], [], [("k", "rp", r4)], ("k", "rp", r4))

            def A_B1(i):
                s, s3 = i % 2, i % 3
                n, c0 = tile_cols(i)
                xnT = fb["xnT"][s3]
                rp = hb["rp"][i % 4]

                def mmkv(e):
                    ins = None
                    for c in range(8):
                        ins = e.matmul(psKV[s][:], lhsT=xnT[:, c, :], rhs=w_in_sb[:, c, 512:1024],
                                       start=(c == 0), stop=(c == 7))
                    return ins
                S.add("pe", mmkv, [("xnT", s3)], [("psKV", s)])

                def mmu(e):
                    ins = None
                    for j in range(4):
                        for c in range(8):
                            ins = e.matmul(psU[s][:, j * 128:(j + 1) * 128],
                                           lhsT=w_in_sb[:, c, 1024 + j * 128:1024 + (j + 1) * 128],
                                           rhs=xnT[:, c, :], start=(c == 0), stop=(c == 7))
                    return ins
                S.add("pe", mmu, [("xnT", s3)], [("psU", s)])
                head_norm_rope(2, s, psKV[s], ("psKV", s), hb, gk_b, "gk_b", rp, "k", rps=i % 4)
                S.add("dve", lambda e: e.tensor_copy(
                    out=VE[:, i, :, 0:128], in_=psKV[s][:, 256:512].rearrange("p (a b) -> p a b", a=2)),
                    [("psKV", s)], [("VE", i)])
                S.add("dve", lambda e: e.tensor_copy(out=ust[s][:].rearrange("p a b -> p (a b)"), in_=psU[s][:]),
                      [("psU", s)], [("ust", s)])
                DMA(u_v[:, :, c0:c0 + n], ust[s][:, :, 0:n], [("ust", s)], [("udram", i)], ("ust", s))

            def A_B2(i):
                s = i % 2
                n, c0 = tile_cols(i)
                kr = hb["kr"][s]

                def trk(e):
                    ins = None
                    for h in range(2):
                        ins = e.transpose(out=psKT[s][:, h * 128:(h + 1) * 128], in_=kr[:, h * 128:(h + 1) * 128],
                                          identity=identb[:])
                    return ins
                S.add("pe", trk, [("k", "kr", s)], [("psKT", s)])
                S.add("dve", lambda e: e.tensor_copy(
                    out=KT[:, :, c0:c0 + n], in_=psKT[s][:, 0:256].rearrange("p (a b) -> p a b", a=2)[:, :, 0:n]),
                    [("psKT", s)], [("KT", i)])

            A_F0(0)
            for t in range(ntile + 3):
                if t + 1 < ntile:
                    A_F0(t + 1)
                if t < ntile:
                    front1(fb, t)
                if 0 <= t - 1 < ntile:
                    front2(fb, t - 1)
                if 0 <= t - 2 < ntile:
                    A_B1(t - 2)
                if 0 <= t - 3 < ntile:
                    A_B2(t - 3)
            if debug in ("A", "A_small"):
                ncol = 16 + (ntile - 1) * 128
                DMA(dbg["kt"].rearrange("p (a b) -> p a b", a=2)[:, :, 0:ncol], KT[:, :, 0:ncol],
                    [("KT", i) for i in range(ntile)], ["dbgkt"], "dbgkt")
                DMA(dbg["v"].rearrange("p (a b) -> p a b", a=NKT)[:, 0:ntile, :],
                    VE[:, 0:ntile, :, :].rearrange("p a b c -> p a (b c)"),
                    [("VE", i) for i in range(ntile)] + ["VE1"], ["dbgv"], "dbgv")
            S.emit()
        if debug in ("A", "A_small"):
            S.final_fence()
            return nc

        with ExitStack() as st:
            fb = alloc_front(st)
            hb = alloc_head(st, 4, "q")
            qst = [sb(st, "qst%d" % i, [128, 4, 128], BF16) for i in range(2)]
            psQ = [ps(st, "psQ%d" % i, [128, 512], F32) for i in range(2)]
            psQT = [ps(st, "psQT%d" % i, [128, 1024], BF16) for i in range(2)]
            def Q_F0(t):
                front0(fb, t, src_rows=I["xown"][t * 128:(t + 1) * 128, :])
                r4 = t % 4
                DMA(hb["rp"][r4][:], I["ropeq"][t * 128:(t + 1) * 128, :], [], [("q", "rp", r4)], ("q", "rp", r4))

            def Q_B1(t):
                s, s3 = t % 2, t % 3
                xnT = fb["xnT"][s3]

                def mmq(e):
                    ins = None
                    for c in range(8):
                        ins = e.matmul(psQ[s][:], lhsT=xnT[:, c, :], rhs=w_in_sb[:, c, 0:512],
                                       start=(c == 0), stop=(c == 7))
                    return ins
                S.add("pe", mmq, [("xnT", s3)], [("psQ", s)])
                head_norm_rope(4, s, psQ[s], ("psQ", s), hb, gq_b, "gq_b", hb["rp"][t % 4], "q", rps=t % 4)

            def Q_B2(t):
                s = t % 2
                kr = hb["kr"][s]

                def trq(e):
                    ins = None
                    for h in range(4):
                        ins = e.transpose(out=psQT[s][:, h * 128:(h + 1) * 128], in_=kr[:, h * 128:(h + 1) * 128],
                                          identity=identb[:])
                    return ins
                S.add("pe", trq, [("q", "kr", s)], [("psQT", s)])
                S.add("dve", lambda e: e.tensor_copy(out=qst[s][:].rearrange("p a b -> p (a b)"), in_=psQT[s][:, 0:512]),
                      [("psQT", s)], [("qst", s)])
                DMA(q_dram[:, :, t * 128:(t + 1) * 128], qst[s][:], [("qst", s)], [("qdram", t // 4)], ("qst", s))

            nq_t = 32 if debug != "C" else 0
            if nq_t:
                Q_F0(0)
            for t in range(nq_t + 3 if nq_t else 0):
                if t + 1 < nq_t:
                    Q_F0(t + 1)
                if t < nq_t:
                    front1(fb, t)
                if 0 <= t - 1 < nq_t:
                    front2(fb, t - 1)
                if 0 <= t - 2 < nq_t:
                    Q_B1(t - 2)
                if 0 <= t - 3 < nq_t:
                    Q_B2(t - 3)
            S.emit()
        st_a.close()
        if debug == "A2":
            S.final_fence()
            return nc

        with ExitStack() as st:
            QT = [sb(st, "QT%d" % i, [128, 4, 512], BF16) for i in range(2)]
            pT = [sb(st, "pT%d" % i, [128, 1024], BF16) for i in range(3)]
            attn = [sb(st, "attn%d" % i, [128, 4, 512], BF16) for i in range(2)]
            mast = [sb(st, "mast%d" % i, [128, 4, 512], BF16) for i in range(2)]
            rec = sb(st, "rec", [128, 4], F32)
            psS = [ps(st, "psS%d" % i, [128, 1024], F32) for i in range(2)]
            psO = [ps(st, "psO%d" % i, [128, 512], F32) for i in range(4)]
            psTa = psS[0][:, 0:512].bitcast(BF16)
            scale = 1.0 / math.sqrt(128.0)
            nqg = 8 if debug not in ("B_small", "D_small") else 1
            if debug == "C":
                nqg = 0

            def kcols(kt):
                return (16, 0) if kt == 0 else (128, 16 + (kt - 1) * 128)

            wstB = [sb(st, "wstB%d" % i, [128, 2048], F32) for i in range(2)]
            wbfB = [sb(st, "wbfB%d" % i, [128, 2048], BF16) for i in range(2)]
            wjobs = []
            for c in range(8):
                wjobs.append((I["w_out"][c * 128:(c + 1) * 128, :], wo_d[:, c, :], goT[:, c:c + 1], 1024, 1))
            for c in range(8):
                for hf in range(2):
                    wjobs.append((I["w_up"][c * 128:(c + 1) * 128, hf * 2048:(hf + 1) * 2048],
                                  wu_d[:, c, hf * 2048:(hf + 1) * 2048], gmlT[:, c:c + 1], 2048, 1))
            wd_v = I["w_down"].rearrange("(c p) n -> p c n", p=128)
            for c2 in range(16):
                wjobs.append((wd_v[:, 2 * c2:2 * c2 + 2, :], wd_d[:, 2 * c2:2 * c2 + 2, :], None, 2048, 2))
            wjob_i = [0]

            def emit_wjob():
                k = wjob_i[0]
                if k >= len(wjobs):
                    return
                wjob_i[0] += 1
                wsrc, wdst, sc, n, a = wjobs[k]
                s = k % 2
                if a == 2:
                    DMA(wstB[s][:].rearrange("p (a b) -> p a b", a=2), wsrc, [], [("wstB", s)], ("wstB", s))
                    S.add("dve", lambda e: e.tensor_copy(out=wbfB[s][:], in_=wstB[s][:]), [("wstB", s)], [("wbfB", s)])
                    DMA(wdst, wbfB[s][:].rearrange("p (a b) -> p a b", a=2), [("wbfB", s)], [("wscr", k)], ("wbfB", s))
                else:
                    DMA(wstB[s][:, 0:n], wsrc, [], [("wstB", s)], ("wstB", s))
                    S.add("dve", lambda e: e.tensor_scalar(out=wbfB[s][:, 0:n], in0=wstB[s][:, 0:n], scalar1=sc, scalar2=None,
                                                           op0=ALU.mult), [("wstB", s)], [("wbfB", s)])
                    DMA(wdst, wbfB[s][:, 0:n], [("wbfB", s)], [("wscr", k)], ("wbfB", s))

            def do_qg(qg):
                qs = qg % 2
                DMA(QT[qs][:], q_dram[:, :, qg * 512:(qg + 1) * 512], [("qdram", qg)], [("QT", qs)], ("QT", qs))
                steps = [(h, tiles) for h in range(4) for tiles in ([[0]] + [[a, a + 1] for a in range(1, NKT, 2)])]
                cnt = 0
                pend = None

                def emit_pv(h, tiles, sl, pl):
                    kv = h // 2

                    def pv(e):
                        ins = None
                        for idx, kt in enumerate(tiles):
                            rows, _ = kcols(kt)
                            for j in range(4):
                                ins = e.matmul(psO[j][:, 0:129],
                                               lhsT=pT[pl][0:rows, idx * 512 + j * 128:idx * 512 + (j + 1) * 128],
                                               rhs=VE[0:rows, kt, kv, :], start=(kt == 0), stop=(kt == NKT - 1))
                        return ins
                    S.add("pe", pv, [("pT", pl)], [("psO", j) for j in range(4)])
                    if tiles[-1] == NKT - 1:
                        S.add("dve", lambda e: [e.reciprocal(out=rec[:, j:j + 1], in_=psO[j][:, 128:129])
                                                for j in range(4)][-1],
                              [("psO", j) for j in range(4)], ["rec"])
                        S.add("dve", lambda e: [e.tensor_scalar(out=attn[qs][:, j, h * 128:(h + 1) * 128],
                                                                in0=psO[j][:, 0:128], scalar1=rec[:, j:j + 1],
                                                                scalar2=None, op0=ALU.mult) for j in range(4)][-1],
                              [("psO", j) for j in range(4)] + ["rec"], [("psO", j) for j in range(4)] + [("attn", qs, h)])

                for (h, tiles) in steps:
                    sl = cnt % 2
                    pl = cnt % 3
                    kv = h // 2

                    def qk(e, sl=sl, kv=kv, h=h, tiles=tiles):
                        ins = None
                        for idx, kt in enumerate(tiles):
                            rows, c0 = kcols(kt)
                            ins = e.matmul(psS[sl][0:rows, idx * 512:(idx + 1) * 512], lhsT=KT[:, kv, c0:c0 + rows],
                                           rhs=QT[qs][:, h, :], start=True, stop=True)
                        return ins
                    S.add("pe", qk, [("QT", qs)], [("psS", sl)])
                    rows = kcols(tiles[0])[0]
                    wdt = 512 * len(tiles)
                    S.add("act", lambda e, sl=sl, pl=pl, rows=rows, wdt=wdt: e.activation(
                        out=pT[pl][0:rows, 0:wdt], in_=psS[sl][0:rows, 0:wdt], func=AF.Exp, scale=scale),
                        [("psS", sl)], [("pT", pl)])
                    if pend is not None:
                        emit_pv(*pend)
                    pend = (h, tiles, sl, pl)
                    cnt += 1
                    if cnt % 50 == 25:
                        emit_wjob()
                emit_pv(*pend)
                ms = qg % 2
                for j in range(4):
                    T = qg * 4 + j
                    ja, jk = jnk(512)
                    S.add("act", lambda e, j=j, T=T, ja=ja: e.activation(out=ja, in_=attn[qs][:, j, :],
                                                                         func=AF.Square, accum_out=ssa[:, T:T + 1]),
                          [("attn", qs, h) for h in range(4)], [("ssa", T), jk])

                    def tra(e, j=j):
                        ins = None
                        for h in range(4):
                            ins = e.transpose(out=psTa[:, h * 128:(h + 1) * 128], in_=attn[qs][:, j, h * 128:(h + 1) * 128],
                                              identity=identb[:])
                        return ins
                    S.add("pe", tra, [("attn", qs, h) for h in range(4)], [("psS", 0)])
                    S.add("dve", lambda e, j=j: e.tensor_copy(
                        out=mast[ms][:, :, j * 128:(j + 1) * 128],
                        in_=psTa[:, 0:512].rearrange("p (a b) -> p a b", a=4)), [("psS", 0)], [("mast", ms, j)])
                DMA(ma_dram[:, :, qg * 512:(qg + 1) * 512], mast[ms][:], [("mast", ms, j) for j in range(4)],
                    [("madram", qg)], ("mast", ms))
            for qg in range(nqg):
                do_qg(qg)
            while wjob_i[0] < len(wjobs):
                emit_wjob()
            if debug in ("B", "B_small"):
                DMA(dbg["st"][:, 0:nqg * 4], ssa[:, 0:nqg * 4], [("ssa", T) for T in range(nqg * 4)], ["dbgst"], "dbgst")
            S.emit()
        st_ab.close()
        if debug in ("B", "B_small"):
            S.final_fence()
            return nc

        with ExitStack() as st:
            w1 = sb(st, "w1", [N1, 2 * N1], BF16)
            dm = sb(st, "dm", [82, 2, N1 * MW], BF16)
            dc = sb(st, "dc", [128, 256], BF16)
            wf32 = sb(st, "wf32", [128, 4, 128], F32)
            wfb = sb(st, "wfb", [128, 4, 128], BF16)
            cw = sb(st, "cw", [64, 4, 2, 256], BF16)
            onesb = sb(st, "onesb", [128, 1], BF16)
            ublk = [sb(st, "ublk%d" % i, [N1, 32, N2], BF16) for i in range(2)]
            A_sb = sb(st, "A_sb", [82, 2, 2, N1, 64], BF16)
            PTQ = sb(st, "PTQ", [64, 2, 2, PTW], BF16)
            yst = [sb(st, "yst%d" % i, [128, 512], BF16) for i in range(2)]
            ysq = [sb(st, "ysq%d" % i, [128, 512], BF16) for i in range(2)]
            NW = 5
            psW = [ps(st, "psW%d" % i, [128, 512], F32) for i in range(NW)]
            ps1 = psW
            ps2 = psW
            psY = [ps(st, "psY%d" % i, [128, 512], F32) for i in range(2)]
            psC = psY[0]
            psSS = ps(st, "psSS", [128, 512], F32)
            DMA(w1[:], I["dftw1"], [], ["w1"], "w1")
            for ch in range(2):
                DMA(dm[:, ch, :], I["dftm"][ch], [], [("dm", ch)], ("dm", ch))
            DMA(dc[:], I["dftc"], [], ["dc"], "dc")
            DMA(wf32[:], I["w_four"].rearrange("g c d -> c g d"), [], ["wf32"], "wf32")
            S.add("dve", lambda e: e.tensor_copy(out=wfb[:], in_=wf32[:]), ["wf32"], ["wfb"])
            S.add("pool", lambda e: e.memset(onesb[:], 1.0), [], ["onesb"])
            for g in range(4):
                for hh in range(2):
                    def mmc(e, g=g, hh=hh):
                        e.matmul(psC[0:64, 0:128], lhsT=dc[:, 64 * hh:64 * hh + 64], rhs=wfb[:, g, :], start=True, stop=True)
                        return e.matmul(psC[0:64, 128:256], lhsT=dc[:, 128 + 64 * hh:128 + 64 * hh + 64], rhs=wfb[:, g, :],
                                        start=True, stop=True)
                    S.add("pe", mmc, ["dc", "wfb"], [("psY", 0)])
                    S.add("dve", lambda e, g=g, hh=hh: e.tensor_copy(out=cw[:, g, hh, 0:128], in_=psC[0:64, 0:128]),
                          [("psY", 0)], [("cw", g, hh, 0)])
                    S.add("dve", lambda e, g=g, hh=hh: e.tensor_scalar(out=cw[:, g, hh, 128:256], in0=psC[0:64, 128:256],
                                                                       scalar1=-1.0, scalar2=None, op0=ALU.mult),
                          [("psY", 0)], [("psY", 0), ("cw", g, hh, 1)])
            u3 = u_dram.rearrange("c (a b) -> a c b", b=N2)
            ev = 0
            p1 = 0
            p2 = 0
            ycnt = 0
            for hg in range(8):
                g, hh = hg // 2, hg % 2
                cbase = g * 128 + hh * 64
                for blk in range(2):
                    bs = (hg * 2 + blk) % 2
                    DMA(ublk[bs][:], u3[:, cbase + blk * 32:cbase + blk * 32 + 32, :],
                        [("udram", i) for i in range(NKT)] if (hg == 0 and blk == 0) else [], [("ublk", bs)], ("ublk", bs))
                    for pair in range(16):
                        for ch in range(2):
                            sl = p1 % NW
                            p1 += 1

                            def mm1(e, bs=bs, pair=pair, ch=ch, sl=sl):
                                ins = None
                                for ci in range(2):
                                    ins = e.matmul(ps1[sl][0:82, ci * 200:(ci + 1) * 200],
                                                   lhsT=ublk[bs][:, pair * 2 + ci, ch * 82:(ch + 1) * 82],
                                                   rhs=w1[:, :], start=True, stop=True)
                                return ins
                            S.add("pe", mm1, [("ublk", bs), "w1"], [("psW", sl)])
                            cl = blk * 32 + pair * 2
                            dst = _ap(A_sb[:, ch, 0, 0, cl:cl + 1], [[N1 * 64, 2], [64, N1], [1, 2]])
                            src = _ap(ps1[sl][0:82, 0:1], [[100, 2], [1, N1], [200, 2]])
                            if ev % 2 == 0:
                                S.add("act", lambda e, dst=dst, src=src: e.activation(out=dst, in_=src, func=AF.Copy),
                                      [("psW", sl)], [("A", hg, ch, cl)])
                            else:
                                S.add("dve", lambda e, dst=dst, src=src: e.tensor_copy(out=dst, in_=src),
                                      [("psW", sl)], [("A", hg, ch, cl)])
                            ev += 1
                akeys = [("A", hg, ch, cl) for ch in range(2) for cl in range(0, 64, 2)]
                k1 = 0
                while k1 < N1:
                    nq = min(6, N1 - k1)
                    sl = (p1 + p2) % NW
                    p2 += 1

                    def mm2(e, k1=k1, nq=nq, sl=sl):
                        ins = None
                        for q in range(nq):
                            kk = k1 + q
                            n = 0
                            for ch in range(2):
                                for ri in range(2):
                                    off = kk * MW + (T2 if ri == 0 else 0)
                                    ins = e.matmul(ps2[sl][0:64, q * 82:(q + 1) * 82], lhsT=A_sb[:, ch, ri, kk, :],
                                                   rhs=dm[:, ch, off:off + 82], start=(n == 0), stop=(n == 3))
                                    n += 1
                        return ins
                    S.add("pe", mm2, akeys + [("dm", 0), ("dm", 1)], [("psW", sl)])
                    for pq in range(2):
                        dst = _ap(PTQ[:, hh, pq, k1:k1 + 1], [[100, T2], [1, nq]])
                        src = _ap(ps2[sl][0:64, pq * T2:pq * T2 + 1], [[1, T2], [82, nq]])
                        eng = "act" if pq == 0 else "dve"
                        if eng == "act":
                            S.add("act", lambda e, dst=dst, src=src: e.activation(out=dst, in_=src, func=AF.Copy),
                                  [("psW", sl)], [("PTQ", hh, pq, k1)])
                        else:
                            S.add("dve", lambda e, dst=dst, src=src: e.tensor_copy(out=dst, in_=src),
                                  [("psW", sl)], [("PTQ", hh, pq, k1)])
                    k1 += nq
                if hh == 1:
                    pkeys = [("PTQ", a, b, c) for a in range(2) for b in range(2) for c in range(0, N1, 6)]
                    for tb in range(8):
                        ys = ycnt % 2
                        ycnt += 1

                        def mmy(e, g=g, tb=tb, ys=ys):
                            ins = None
                            n = 0
                            for h2 in range(2):
                                for pq in range(2):
                                    ins = e.matmul(psY[ys][:], lhsT=cw[:, g, h2, pq * 128:(pq + 1) * 128],
                                                   rhs=PTQ[:, h2, pq, tb * 512:(tb + 1) * 512], start=(n == 0), stop=(n == 3))
                                    n += 1
                            return ins
                        S.add("pe", mmy, pkeys + [("cw", g, a, b) for a in range(2) for b in range(2)], [("psY", ys)])
                        S.add("act", lambda e, ys=ys: e.activation(out=yst[ys][:], in_=psY[ys][:], func=AF.Copy),
                              [("psY", ys)], [("yst", ys)])
                        S.add("act", lambda e, ys=ys: e.activation(out=ysq[ys][:], in_=psY[ys][:], func=AF.Square),
                              [("psY", ys)], [("ysq", ys)])

                        def mms(e, ys=ys, tb=tb):
                            ins = None
                            for j in range(4):
                                T = tb * 4 + j
                                ins = e.matmul(psSS[:, T:T + 1], lhsT=ysq[ys][:, j * 128:(j + 1) * 128], rhs=onesb[:, 0:1],
                                               start=True, stop=True)
                            return ins
                        S.add("pe", mms, [("ysq", ys), "onesb"], ["psSS"])
                        DMA(mf_dram[:, g, tb * 512:(tb + 1) * 512], yst[ys][:], [("yst", ys)], [("mfdram", g, tb)], ("yst", ys))
                    S.add("dve", lambda e, g=g: e.tensor_copy(out=ssfp[:, g, :], in_=psSS[:, 0:32]), ["psSS"],
                          ["psSS", ("ssfp", g)])
            S.emit()
        if debug == "C":
            DMA(dbg["st"][:, 0:128], ssfp[:].rearrange("p a b -> p (a b)"), [], ["dbgst"], "dbgst")
            S.emit()
            S.final_fence()
            return nc

        st_d = top.enter_context(ExitStack())
        wo = sb(st_d, "wo", [128, 8, 1024], BF16)
        wu = sb(st_d, "wu", [128, 8, 4096], BF16)
        wd = sb(st_d, "wd", [128, 32, 1024], BF16)
        gfin_b = sb(st_d, "gfin_b", [128, 1024], F32)
        ra = sb(st_d, "ra", [128, 32], F32)
        rf = sb(st_d, "rf", [128, 32], F32)
        with ExitStack() as st:
            tmpa = sb(st, "tmpa", [128, 32], F32)
            tmpf = sb(st, "tmpf", [128, 32], F32)
            DMA(gfin_b[:], I["g_final"].partition_broadcast(128), [], ["gfin_b"], "gfin_b")
            rms_rstd(ssa[:], tmpa[:], ra[:], 512.0, "ssa_all", "tmpa", "ra")
            S.add("dve", lambda e: e.tensor_tensor(out=tmpf[:], in0=ssfp[:, 0, :], in1=ssfp[:, 1, :], op=ALU.add), [], ["tf0"])
            S.add("dve", lambda e: e.tensor_tensor(out=tmpf[:], in0=tmpf[:], in1=ssfp[:, 2, :], op=ALU.add), ["tf0"], ["tf1"])
            S.add("dve", lambda e: e.tensor_tensor(out=tmpf[:], in0=tmpf[:], in1=ssfp[:, 3, :], op=ALU.add), ["tf1"], ["tf2"])
            rms_rstd(tmpf[:], tmpf[:], rf[:], 512.0, "tf2", "tf3", "rf")
            for c in range(0, 8, 4):
                DMA(wo[:, c:c + 4, :], wo_d[:, c:c + 4, :], [], [("wo", c)], ("wo", c))
            for c in range(8):
                DMA(wu[:, c, :], wu_d[:, c, :], [], [("wu", c)], ("wu", c))
            for c in range(0, 32, 4):
                DMA(wd[:, c:c + 4, :], wd_d[:, c:c + 4, :], [], [("wd", c)], ("wd", c))
            S.emit()

        with ExitStack() as st:
            xg = [sb(st, "xg%d" % i, [128, 2, 1024], F32) for i in range(2)]
            mat = [sb(st, "mat%d" % i, [128, 4, 256], BF16) for i in range(2)]
            mft = [sb(st, "mft%d" % i, [128, 4, 256], BF16) for i in range(2)]
            mb = [sb(st, "mb%d" % i, [128, 1024], BF16) for i in range(2)]
            mT = [sb(st, "mT%d" % i, [128, 8, 256], BF16) for i in range(2)]
            aT = [sb(st, "aT%d" % i, [128, 256], BF16) for i in range(5)]
            aT2 = [sb(st, "aT2_%d" % i, [128, 256], BF16) for i in range(5)]
            stt = sb(st, "stt", [128, 8], F32)
            psAF = ps(st, "psAF", [128, 1024], F32)
            psT = psAF[:, 0:512].bitcast(BF16)
            psH = [ps(st, "psH%d" % i, [128, 512], F32) for i in range(2)]
            psD = [ps(st, "psD%d" % i, [128, 1024], F32) for i in range(2)]
            NG = 16 if debug != "D_small" else 2

            def P1_load(G):
                xs = G % 2
                DMA(xg[xs][:], I["xown"][G * 256:(G + 1) * 256, :].rearrange("(j p) f -> p j f", p=128), [],
                    [("xg", xs, 0), ("xg", xs, 1)], ("xg", xs))
                DMA(mat[xs][:], ma_dram[:, :, G * 256:(G + 1) * 256], [], [("mat", xs)], ("mat", xs))
                DMA(mft[xs][:], mf_dram[:, :, G * 256:(G + 1) * 256], [], [("mft", xs)], ("mft", xs))

            def P1_part(G, j, which):
                xs = G % 2
                T = 2 * G + j
                src, key, rr = ((mat, "mat", ra), (mft, "mft", rf))[which]

                def mmo(e):
                    ins = None
                    for hf in range(2):
                        for c in range(4):
                            ins = e.matmul(psAF[:, hf * 512:(hf + 1) * 512], lhsT=src[xs][:, c, j * 128:(j + 1) * 128],
                                           rhs=wo[:, which * 4 + c, hf * 512:(hf + 1) * 512], start=(c == 0), stop=(c == 3))
                    return ins
                S.add("pe", mmo, [(key, xs)], ["psAF"])
                S.add("dve", lambda e: e.scalar_tensor_tensor(
                    out=xg[xs][:, j, :], in0=psAF[:], scalar=rr[:, T:T + 1], in1=xg[xs][:, j, :],
                    op0=ALU.mult, op1=ALU.add), ["psAF", ("xg", xs, j)], ["psAF", ("xg", xs, j)])
                if which == 1:
                    ja, jk = jnk()
                    S.add("act", lambda e: e.activation(out=ja, in_=xg[xs][:, j, :], func=AF.Square,
                                                        accum_out=stt[:, j:j + 1]), [("xg", xs, j)], [("stt", j), jk])
                    rms_rstd(stt[:, j:j + 1], stt[:, 2 + j:3 + j], stt[:, 4 + j:5 + j], 1024.0, ("stt", j), ("stt", 2 + j), ("stt", 4 + j))
                    S.add("dve", lambda e: e.tensor_scalar(out=mb[j][:], in0=xg[xs][:, j, :], scalar1=stt[:, 4 + j:5 + j],
                                                           scalar2=None, op0=ALU.mult),
                          [("xg", xs, j), ("stt", 4 + j)], [("mb", j)])

            def P2_part(G, j):
                xs = G % 2

                def trm(e):
                    ins = None
                    for c in range(8):
                        ins = e.transpose(out=psT[:, c * 128:(c + 1) * 128], in_=mb[j][:, c * 128:(c + 1) * 128],
                                          identity=identb[:])
                    return ins
                S.add("pe", trm, [("mb", j)], ["psAF"])
                S.add("act", lambda e: e.activation(out=mT[xs][:, :, j * 128:(j + 1) * 128],
                                                    in_=psT.rearrange("p (a b) -> p a b", a=8), func=AF.Copy),
                      ["psAF"], [("mT", xs, j)])

            def P1(G):
                P1_load(G)
                for j in range(2):
                    P1_part(G, j, 0)
                    P1_part(G, j, 1)

            def P2(G):
                for j in range(2):
                    P2_part(G, j)

            SCHED = {2: lambda G: P1_load(G), 4: lambda G: P1_part(G, 0, 0), 8: lambda G: P1_part(G, 0, 1),
                     12: lambda G: P1_part(G, 1, 0), 16: lambda G: P1_part(G, 1, 1),
                     21: lambda G: P2_part(G, 0), 26: lambda G: P2_part(G, 1)}

            def down(G, f, sl):
                def mmd(e):
                    ins = None
                    for j in range(2):
                        for hf in range(2):
                            ins = e.matmul(psD[j][:, hf * 512:(hf + 1) * 512], lhsT=aT2[sl][:, j * 128:(j + 1) * 128],
                                           rhs=wd[:, f, hf * 512:(hf + 1) * 512], start=(f == 0), stop=(f == 31))
                    return ins
                S.add("pe", mmd, [("aT2", sl)], [("psD", 0), ("psD", 1)])

            P1(0)
            P2(0)
            fcnt = 0
            for G in range(NG):
                xs = G % 2
                pend = []
                for f in range(32):
                    hs = fcnt % 2
                    sl = fcnt % 5
                    fcnt += 1

                    def mmu2(e, f=f, hs=hs, xs=xs):
                        ins = None
                        for c in range(8):
                            ins = e.matmul(psH[hs][:, 0:256], lhsT=wu[:, c, f * 128:(f + 1) * 128],
                                           rhs=mT[xs][:, c, :], start=(c == 0), stop=(c == 7))
                        return ins
                    S.add("pe", mmu2, [("mT", xs, 0), ("mT", xs, 1)], [("psH", hs)])
                    S.add("act", lambda e, hs=hs, sl=sl: e.activation(out=aT[sl][:], in_=psH[hs][:, 0:256],
                                                                      func=AF.Relu), [("psH", hs)], [("aT", sl)])
                    S.add("pool", lambda e, sl=sl: e.tensor_tensor(out=aT2[sl][:], in0=aT[sl][:], in1=aT[sl][:], op=ALU.mult),
                          [("aT", sl)], [("aT2", sl)])
                    pend.append((f, sl))
                    if len(pend) > 3:
                        down(G, *pend.pop(0))
                    if f in SCHED and G + 1 < NG:
                        SCHED[f](G + 1)
                while pend:
                    down(G, *pend.pop(0))
                for j in range(2):
                    S.add("dve", lambda e, j=j, xs=xs: e.tensor_tensor(out=xg[xs][:, j, :], in0=psD[j][:], in1=xg[xs][:, j, :], op=ALU.add),
                          [("psD", j), ("xg", xs, j)], [("psD", j), ("xg", xs, j)])
                for j in range(2):
                    ja, jk = jnk()
                    S.add("act", lambda e, j=j, xs=xs, ja=ja: e.activation(out=ja, in_=xg[xs][:, j, :], func=AF.Square,
                                                                           accum_out=stt[:, 6 + j:7 + j]),
                          [("xg", xs, j)], [("stt", 6 + j), jk])
                    rms_rstd(stt[:, 6 + j:7 + j], stt[:, 6 + j:7 + j], stt[:, 6 + j:7 + j], 1024.0,
                             ("stt", 6 + j), ("stt", 6 + j, "b"), ("stt", 6 + j, "c"))
                    S.add("dve", lambda e, j=j, xs=xs: e.scalar_tensor_tensor(
                        out=xg[xs][:, j, :], in0=xg[xs][:, j, :], scalar=stt[:, 6 + j:7 + j], in1=gfin_b[:],
                        op0=ALU.mult, op1=ALU.mult), [("xg", xs, j), ("stt", 6 + j, "c"), "gfin_b"], [("xg", xs, j)])
                DMA(out[G * 256:(G + 1) * 256, :].rearrange("(j p) f -> p j f", p=128), xg[xs][:],
                    [("xg", xs, 0), ("xg", xs, 1)], [("out", G)], ("xg", xs))
            S.emit()
        S.final_fence()
    return nc


def _rope_tables(rows, cols):
    inv = (np.float32(10000.0) ** (-np.arange(0, 64, 2, dtype=np.float32) / np.float32(64))).astype(np.float32)
    out = np.zeros((rows.shape[0], 256), np.float32)
    for part, pos in ((0, rows), (1, cols)):
        ang = (pos.astype(np.float32)[:, None] * inv[None, :]).astype(np.float32)
        c = np.cos(ang).astype(np.float32)
        s = np.sin(ang).astype(np.float32)
        b = part * 64
        out[:, b:b + 32] = c
        out[:, b + 32:b + 64] = c
        out[:, 128 + b:128 + b + 32] = -s
        out[:, 128 + b + 32:128 + b + 64] = s
    return out


def _positions():
    rows = np.concatenate([np.full(N_META, -1.0), np.repeat(np.arange(SEQ // 64), 64)]).astype(np.float32)
    cols = np.concatenate([np.arange(N_META), np.tile(np.arange(64), SEQ // 64)]).astype(np.float32)
    return rows, cols


def _dft_tables(j):
    bf = ml_dtypes.bfloat16
    k0 = N_META + OWN * j
    n1 = np.arange(N1)[:, None]
    k1p = np.arange(N1)[None, :]
    k1 = (k1p + k0) % N1
    ang = 2.0 * np.pi * ((n1 * k1) % N1) / N1
    w1 = np.concatenate([np.cos(ang), np.sin(ang)], axis=1).astype(bf)
    n2 = np.arange(N2, dtype=np.int64)[:, None, None]
    kk = (k0 + np.arange(N1, dtype=np.int64)[None, :, None] + N1 * np.arange(T2, dtype=np.int64)[None, None, :])
    th = 2.0 * np.pi * ((n2 * kk) % LTOT) / LTOT
    m = np.concatenate([-np.sin(th), np.cos(th), np.sin(th)], axis=2)
    m = m.reshape(2, 82, N1 * MW).astype(bf)
    return w1, m


def _dftc():
    c = np.arange(128)
    ang = 2.0 * np.pi * ((c[:, None] * c[None, :]) % 128) / 128.0
    sc = 1.0 / math.sqrt(128.0 * LTOT)
    return np.concatenate([np.cos(ang) * sc, np.sin(ang) * sc], axis=1).astype(ml_dtypes.bfloat16)


_CACHE = {}


def _in_maps(x, meta_tokens, g_mix, w_in, g_q, g_k, w_fourier, g_attn_out, g_fourier_out, w_out, g_mlp, w_up,
             w_down, g_final, cores):
    f = np.float32
    rows, cols = _positions()
    ropefull = _rope_tables(rows, cols)
    ropek = np.zeros((NKT * 128, 256), f)
    ropek[0:16] = ropefull[0:16]
    ropek[128:] = ropefull[16:]
    ident = np.eye(128, dtype=f).astype(ml_dtypes.bfloat16)
    dftc = _dftc()
    common = dict(
        meta=np.ascontiguousarray(meta_tokens, f), w_in=np.ascontiguousarray(w_in[0], f),
        w_out=np.ascontiguousarray(w_out[0], f), w_up=np.ascontiguousarray(w_up[0], f),
        w_down=np.ascontiguousarray(w_down[0], f), g_mix=np.ascontiguousarray(g_mix[0], f),
        g_q=np.ascontiguousarray(g_q[0], f), g_k=np.ascontiguousarray(g_k[0], f),
        g_ao=np.ascontiguousarray(g_attn_out[0], f), g_fo=np.ascontiguousarray(g_fourier_out[0], f),
        g_mlp=np.ascontiguousarray(g_mlp[0], f), g_final=np.ascontiguousarray(g_final, f),
        w_four=np.ascontiguousarray(w_fourier[0], f), ropek=ropek, ident=ident, dftc=dftc)
    maps = []
    for c in cores:
        b, j = c // 4, c % 4
        w1, m = _dft_tables(j)
        d = dict(common)
        d["xb"] = np.ascontiguousarray(x[b], f)
        d["xown"] = np.ascontiguousarray(x[b, OWN * j:OWN * (j + 1)], f)
        d["ropeq"] = np.ascontiguousarray(ropefull[16 + OWN * j:16 + OWN * (j + 1)])
        d["dftw1"] = w1
        d["dftm"] = m
        maps.append(d)
    return maps


def kernel(x, meta_tokens, g_mix, w_in, g_q, g_k, w_fourier, g_attn_out, g_fourier_out, w_out, g_mlp, w_up,
           w_down, g_final):
    cores = list(range(8))
    maps = _in_maps(x, meta_tokens, g_mix, w_in, g_q, g_k, w_fourier, g_attn_out, g_fourier_out, w_out, g_mlp,
                    w_up, w_down, g_final, cores)
    if "nc" not in _CACHE:
        _CACHE["nc"] = build_program()
    res = run_bass_kernel_spmd(_CACHE["nc"], maps, core_ids=cores)
    outp = np.zeros((2, SEQ, D_MODEL), np.float32)
    for c in cores:
        b, j = c // 4, c % 4
        outp[b, OWN * j:OWN * (j + 1)] = np.asarray(res.results[c]["out"], np.float32)
    return outp
```
